# Optimizing a Trainium2 kernel written in Bass

```python
import jax
import jax.numpy as jnp
from jax import lax
import numpy as np

D_MODEL = 1024
BATCH = 16
SEQ = 2048
DEPTH = 2

GRID_W = 64
CTX_LEN = 256
D_MIX = D_MODEL
HEAD_DIM = 64
RW_W = D_MIX // 4
RW_H = RW_W // HEAD_DIM
RW_DH = HEAD_DIM
RW_LW = 64
RW_LA = 64
RW_LG = 128
NA_W = D_MIX // 2
NA_H = NA_W // HEAD_DIM
NA_KR = 8
NA_KC = 16
SC_W = D_MIX - RW_W - NA_W
SC_K = 3
PEER_H = 8
PEER_NKEYS = 128
PEER_N = PEER_NKEYS * PEER_NKEYS
PEER_DQ = 256
PEER_TOPK = 16
PEER_CHUNK = 128

NORM_EPS = 1e-6
GN_EPS = 64e-5
NEG_INF = -1e30
IN_SPLITS = (RW_W, RW_W, RW_W, 2 * RW_LW, 2 * RW_LA, RW_LG, NA_W, NA_W, NA_W, SC_W, SC_W, SC_W)
D_IN = sum(IN_SPLITS)

kernel_name = 'hybrid_rwkv7_natten_shortconv_peer_dit'


def rmsnorm(x, g):
    xf = x.astype(jnp.float32)
    y = xf * lax.rsqrt(jnp.mean(xf * xf, axis=-1, keepdims=True) + NORM_EPS)
    return (y * g.astype(jnp.float32)).astype(x.dtype)


def split_columns(z):
    offsets = [int(o) for o in np.cumsum(IN_SPLITS)[:-1]]
    return jnp.split(z, offsets, axis=-1)


def short_conv3(u, w):
    up = jnp.pad(u, ((0, 0), (1, 1), (0, 0)))
    return up[:, :-2] * w[0] + up[:, 1:-1] * w[1] + up[:, 2:] * w[2]


def gated_short_conv(z_b, z_c, z_x, w):
    return z_b * short_conv3(z_c * z_x, w)


def rwkv_prepare(z_r, z_k, z_v, z_w, z_a, w0, w_up, a0, a_up, k_k, k_a):
    f32 = jnp.float32
    B, T, _ = z_r.shape
    r, k, v = z_r.astype(f32), z_k.astype(f32), z_v.astype(f32)
    lw = jnp.tanh(z_w.astype(f32).reshape(B, T, 2, RW_LW))
    la = z_a.astype(f32).reshape(B, T, 2, RW_LA)
    w_raw = w0 + jnp.einsum('btdr,drc->btdc', lw, w_up)
    decay = jnp.exp(-jnp.exp(-jax.nn.softplus(-w_raw) - 0.5))
    a = jax.nn.sigmoid(a0 + jnp.einsum('btdr,drc->btdc', la, a_up))
    kk = (k * k_k).reshape(B, T, RW_H, RW_DH)
    kk = (kk * lax.rsqrt(jnp.sum(kk * kk, axis=-1, keepdims=True) + 1e-12)).reshape(B, T, RW_W)
    k_dir = k[:, :, None, :] * (1.0 + (a - 1.0) * k_a)
    b_dir = kk[:, :, None, :] * a
    tm = lambda t: jnp.moveaxis(t.reshape(B, T, RW_H, RW_DH), 1, 0)
    common = (tm(r), tm(kk), tm(v))
    per_dir = tuple((tm(decay[:, :, d]), tm(b_dir[:, :, d]), tm(k_dir[:, :, d])) for d in range(2))
    return common, per_dir


def wkv_scan(state0, r, kk, v, decay, b, k, reverse):
    def step(S, inp):
        r_t, kk_t, v_t, w_t, b_t, k_t = inp
        sa = jnp.einsum('bhij,bhj->bhi', S, kk_t)
        S = S * w_t[:, :, None, :] - sa[..., None] * b_t[:, :, None, :] + v_t[..., None] * k_t[:, :, None, :]
        return S, jnp.einsum('bhij,bhj->bhi', S, r_t)
    return lax.scan(step, state0, (r, kk, v, decay, b, k), reverse=reverse)


def rwkv_output(y_tm, z_r, z_k, z_v, z_g, g_up, r_k, lnx_g):
    f32 = jnp.float32
    B, T, _ = z_r.shape
    y = jnp.moveaxis(y_tm, 0, 1)
    mu = jnp.mean(y, axis=-1, keepdims=True)
    var = jnp.mean(jnp.square(y - mu), axis=-1, keepdims=True)
    y = (y - mu) * lax.rsqrt(var + GN_EPS) * lnx_g.astype(f32).reshape(RW_H, RW_DH)
    hd = lambda t: t.astype(f32).reshape(B, T, RW_H, RW_DH)
    r, k, v = hd(z_r), hd(z_k), hd(z_v)
    y = y + jnp.sum(r * k * r_k, axis=-1, keepdims=True) * v
    gate = jax.nn.sigmoid(z_g.astype(f32)) @ g_up.astype(f32)
    return (y.reshape(B, T, RW_W) * gate).astype(z_r.dtype)


def rwkv_mixer(pc, pl, w0, w_up, a0, a_up, g_up, k_k, k_a, r_k, lnx_g, need_ctx):
    c_common, c_dirs = rwkv_prepare(*pc[:5], w0, w_up, a0, a_up, k_k, k_a)
    l_common, l_dirs = rwkv_prepare(*pl[:5], w0, w_up, a0, a_up, k_k, k_a)
    state0 = jnp.zeros((pl[0].shape[0], RW_H, RW_DH, RW_DH), jnp.float32)
    ys_c, ys_l = [], []
    for d, reverse in enumerate((False, True)):
        ctx_state, y_c = wkv_scan(state0, *c_common, *c_dirs[d], reverse=reverse)
        _, y_l = wkv_scan(ctx_state, *l_common, *l_dirs[d], reverse=reverse)
        ys_c.append(y_c)
        ys_l.append(y_l)
    out_l = rwkv_output(ys_l[0] + ys_l[1], pl[0], pl[1], pl[2], pl[5], g_up, r_k, lnx_g)
    out_c = rwkv_output(ys_c[0] + ys_c[1], pc[0], pc[1], pc[2], pc[5], g_up, r_k, lnx_g) if need_ctx else None
    return out_c, out_l


def neighbourhood_attention(ql, kl, vl, kc, vc, rpb):
    f32 = jnp.float32
    B, S = ql.shape[:2]
    rows = S // GRID_W
    kr = min(NA_KR, rows)
    scale = HEAD_DIM ** -0.5
    grid = lambda t: t.reshape(B, rows, GRID_W, NA_H, HEAD_DIM)
    qg = grid(ql).astype(f32) * scale
    kg, vg = grid(kl).astype(f32), grid(vl)
    ri = jnp.arange(rows)
    row_idx = jnp.clip(ri - kr // 2, 0, rows - kr)[:, None] + jnp.arange(kr)[None, :]
    ci = jnp.arange(GRID_W)
    c0 = jnp.clip(ci - NA_KC // 2, 0, GRID_W - NA_KC)
    col_ok = (ci[None, :] >= c0[:, None]) & (ci[None, :] < c0[:, None] + NA_KC)
    dr = row_idx - ri[:, None] + (NA_KR - 1)
    dc = jnp.clip(ci[None, :] - ci[:, None] + (NA_KC - 1), 0, 2 * NA_KC - 2)
    bias = rpb[:, dr[:, None, :, None], dc[None, :, None, :]].astype(f32)
    k_band = kg[:, row_idx]
    v_band = vg[:, row_idx]
    s_win = jnp.einsum('brqhd,brkwhd->bhrqkw', qg, k_band) + bias[None]
    s_win = jnp.where(col_ok[:, None, :], s_win, NEG_INF)
    s_ctx = jnp.einsum('brqhd,bnhd->bhrqn', qg, kc.astype(f32))
    n_win = kr * GRID_W
    s = jnp.concatenate([s_win.reshape(B, NA_H, rows, GRID_W, n_win), s_ctx], axis=-1)
    p = jax.nn.softmax(s, axis=-1)
    p_win = p[..., :n_win].reshape(B, NA_H, rows, GRID_W, kr, GRID_W).astype(vl.dtype)
    p_ctx = p[..., n_win:].astype(vl.dtype)
    out = jnp.einsum('bhrqkw,brkwhd->brqhd', p_win, v_band) + jnp.einsum('bhrqn,bnhd->brqhd', p_ctx, vc)
    return out.reshape(B, S, NA_W)


def context_attention(qc, kc, vc):
    f32 = jnp.float32
    B, C = qc.shape[:2]
    s = jnp.einsum('bqhd,bkhd->bhqk', qc.astype(f32), kc.astype(f32)) * (HEAD_DIM ** -0.5)
    p = jax.nn.softmax(s, axis=-1).astype(vc.dtype)
    return jnp.einsum('bhqk,bkhd->bqhd', p, vc).reshape(B, C, NA_W)


def token_mixing(hc, hl, w_in, w_out, rw_w0, rw_w_up, rw_a0, rw_a_up, rw_g_up, rw_k_k, rw_k_a,
                 rw_r_k, rw_lnx_g, na_rpb, sc_conv_w, need_ctx):
    B, S = hl.shape[:2]
    C = hc.shape[1]
    pc = split_columns(hc @ w_in)
    pl = split_columns(hl @ w_in)
    y_rw_c, y_rw_l = rwkv_mixer(pc, pl, rw_w0, rw_w_up, rw_a0, rw_a_up, rw_g_up, rw_k_k, rw_k_a,
                                rw_r_k, rw_lnx_g, need_ctx)
    qc, kc, vc = (t.reshape(B, C, NA_H, HEAD_DIM) for t in pc[6:9])
    ql, kl, vl = (t.reshape(B, S, NA_H, HEAD_DIM) for t in pl[6:9])
    y_na_l = neighbourhood_attention(ql, kl, vl, kc, vc, na_rpb)
    y_sc_l = gated_short_conv(*pl[9:12], sc_conv_w)
    yl = jnp.concatenate([y_rw_l, y_na_l, y_sc_l], axis=-1) @ w_out
    if not need_ctx:
        return None, yl
    y_na_c = context_attention(qc, kc, vc)
    y_sc_c = gated_short_conv(*pc[9:12], sc_conv_w)
    yc = jnp.concatenate([y_rw_c, y_na_c, y_sc_c], axis=-1) @ w_out
    return yc, yl


def peer_ffn(h, q_w, sub_keys, u_tab, v_tab):
    f32 = jnp.float32
    lead = h.shape[:-1]
    t = h.reshape(-1, D_MODEL)
    n = t.shape[0]
    q = (t @ q_w).reshape(n, PEER_H, 2, PEER_DQ // 2)
    s = jnp.einsum('thpd,hpnd->thpn', q.astype(f32), sub_keys.astype(f32))
    s1, i1 = lax.top_k(s[:, :, 0], PEER_TOPK)
    s2, i2 = lax.top_k(s[:, :, 1], PEER_TOPK)
    cand_s = (s1[..., :, None] + s2[..., None, :]).reshape(n, PEER_H, PEER_TOPK * PEER_TOPK)
    cand_i = (i1[..., :, None] * PEER_NKEYS + i2[..., None, :]).reshape(n, PEER_H, PEER_TOPK * PEER_TOPK)
    top_s, pos = lax.top_k(cand_s, PEER_TOPK)
    idx = jnp.take_along_axis(cand_i, pos, axis=-1)
    gates = jax.nn.softmax(top_s, axis=-1).astype(h.dtype)

    def chunk(args):
        x_c, i_c, g_c = args
        act = jax.nn.gelu(jnp.einsum('chkd,cd->chk', u_tab[i_c], x_c))
        return jnp.einsum('chk,chkd->cd', g_c * act, v_tab[i_c])

    nc = n // PEER_CHUNK
    out = lax.map(chunk, (t.reshape(nc, PEER_CHUNK, D_MODEL),
                          idx.reshape(nc, PEER_CHUNK, PEER_H, PEER_TOPK),
                          gates.reshape(nc, PEER_CHUNK, PEER_H, PEER_TOPK)))
    return out.reshape(lead + (D_MODEL,))


def setup_inputs(seed: int = 0) -> dict:
    key = jax.random.key(seed)
    keys = iter(jax.random.split(key, 32))
    nrm = lambda shape, std: std * jax.random.normal(next(keys), shape, jnp.float32)
    L, D = DEPTH, D_MODEL
    return {
        'x': nrm((BATCH, SEQ, D), 1.0),
        'c': nrm((BATCH, D), 1.0),
        'ctx': nrm((BATCH, CTX_LEN, D), 1.0),
        'c_ctx': nrm((D,), 1.0),
        'ada_w': nrm((L, D, 6 * D), 0.5 * D ** -0.5),
        'ada_b': nrm((L, 6 * D), 0.02),
        'norm1_g': 1.0 + nrm((L, D), 0.02),
        'norm2_g': 1.0 + nrm((L, D), 0.02),
        'w_in': nrm((L, D, D_IN), D ** -0.5),
        'rw_w0': jnp.linspace(-5.0, 1.0, RW_W, dtype=jnp.float32) + nrm((L, 2, RW_W), 0.1),
        'rw_w_up': nrm((L, 2, RW_LW, RW_W), 0.5 * RW_LW ** -0.5),
        'rw_a0': nrm((L, 2, RW_W), 0.1),
        'rw_a_up': nrm((L, 2, RW_LA, RW_W), 0.5 * RW_LA ** -0.5),
        'rw_g_up': nrm((L, RW_LG, RW_W), RW_LG ** -0.5),
        'rw_k_k': 0.85 + nrm((L, RW_W), 0.05),
        'rw_k_a': 1.0 + nrm((L, RW_W), 0.05),
        'rw_r_k': nrm((L, RW_H, RW_DH), 0.1),
        'rw_lnx_g': 1.0 + nrm((L, RW_W), 0.02),
        'na_rpb': nrm((L, NA_H, 2 * NA_KR - 1, 2 * NA_KC - 1), 0.1),
        'sc_conv_w': nrm((L, SC_K, SC_W), SC_K ** -0.5),
        'w_out': nrm((L, D_MIX, D), D_MIX ** -0.5),
        'peer_q_w': nrm((L, D, PEER_H * PEER_DQ), D ** -0.5),
        'peer_sub_keys': nrm((L, PEER_H, 2, PEER_NKEYS, PEER_DQ // 2), (PEER_DQ // 2) ** -0.5),
        'peer_u': nrm((L, PEER_N, D), D ** -0.5),
        'peer_v': nrm((L, PEER_N, D), PEER_TOPK ** -0.5),
        'final_g': 1.0 + nrm((D,), 0.02),
    }


def reference(x, c, ctx, c_ctx, ada_w, ada_b, norm1_g, norm2_g, w_in, rw_w0, rw_w_up, rw_a0,
              rw_a_up, rw_g_up, rw_k_k, rw_k_a, rw_r_k, rw_lnx_g, na_rpb, sc_conv_w, w_out,
              peer_q_w, peer_sub_keys, peer_u, peer_v, final_g):
    xl, xc = x, ctx
    for i in range(DEPTH):
        need_ctx = i < DEPTH - 1
        mod_l = jax.nn.silu(c) @ ada_w[i] + ada_b[i]
        mod_c = jax.nn.silu(c_ctx) @ ada_w[i] + ada_b[i]
        sh1, sc1, g1, sh2, sc2, g2 = jnp.split(mod_l[:, None, :], 6, axis=-1)
        csh1, csc1, cg1, csh2, csc2, cg2 = jnp.split(mod_c, 6, axis=-1)
        hl = rmsnorm(xl, norm1_g[i]) * (1.0 + sc1) + sh1
        hc = rmsnorm(xc, norm1_g[i]) * (1.0 + csc1) + csh1
        yc, yl = token_mixing(hc, hl, w_in[i], w_out[i], rw_w0[i], rw_w_up[i], rw_a0[i], rw_a_up[i],
                              rw_g_up[i], rw_k_k[i], rw_k_a[i], rw_r_k[i], rw_lnx_g[i], na_rpb[i],
                              sc_conv_w[i], need_ctx)
        xl = xl + g1 * yl
        hl2 = rmsnorm(xl, norm2_g[i]) * (1.0 + sc2) + sh2
        xl = xl + g2 * peer_ffn(hl2, peer_q_w[i], peer_sub_keys[i], peer_u[i], peer_v[i])
        if need_ctx:
            xc = xc + cg1 * yc
            hc2 = rmsnorm(xc, norm2_g[i]) * (1.0 + csc2) + csh2
            xc = xc + cg2 * peer_ffn(hc2, peer_q_w[i], peer_sub_keys[i], peer_u[i], peer_v[i])
    return rmsnorm(xl, final_g)
```

```python
import numpy as np
import ml_dtypes
from contextlib import ExitStack
import concourse.bass as bass
import concourse.mybir as mybir
from concourse.bass_utils import run_bass_kernel_spmd

F32 = mybir.dt.float32
BF16 = mybir.dt.bfloat16
U32 = mybir.dt.uint32
ALU = mybir.AluOpType
AF = mybir.ActivationFunctionType
AX = mybir.AxisListType

ENGS = ['tensor', 'vector', 'scalar', 'gpsimd', 'sync']
NDMA = 12

D = 1024
DEPTH = 2
LB = 2
CTX = 256
SEQ = 2048
L = CTX + SEQ
T = LB * L
NT = T // 128
TPB = L // 128
D_IN = 3456
NEXP = 16384
NEG = -30000.0
NORM_EPS = 1e-6
GN_EPS = 64e-5


class Prog:
    def __init__(self, nc, stack):
        self.nc = nc
        self.sems = {e: stack.enter_context(nc.semaphore("s_" + e)) for e in ENGS}
        self.cnt = {e: 0 for e in ENGS}
        self.dsem = [stack.enter_context(nc.semaphore("d%d" % i)) for i in range(NDMA)]
        self.dcnt = [0] * NDMA
        self.dnext = 0
        self.known = {e: {} for e in ENGS}
        self.last_w = {}
        self.readers = {}
        self.pending = {e: [] for e in ENGS}
        self.nins = 0

    def _sem(self, sk):
        return self.sems[sk[1]] if sk[0] == 'e' else self.dsem[sk[1]]

    def _deps(self, eng, reads, writes, extra=None):
        deps = {}

        def add(sk, v):
            if v > deps.get(sk, 0):
                deps[sk] = v
        for k in reads:
            lw = self.last_w.get(k)
            if lw:
                add(*lw)
        for k in writes:
            lw = self.last_w.get(k)
            if lw:
                add(*lw)
            for sk, v in self.readers.get(k, {}).items():
                add(sk, v)
        if extra:
            for sk, v in extra:
                add(sk, v)
        out = []
        kn = self.known[eng]
        for sk, v in deps.items():
            if sk == ('e', 'tensor') and eng == 'tensor':
                continue
            if kn.get(sk, 0) >= v:
                continue
            kn[sk] = v
            out.append((sk, v))
        return out

    def _mark(self, mysk, myval, reads, writes):
        for k in reads:
            self.readers.setdefault(k, {})[mysk] = myval
        for k in writes:
            self.last_w[k] = (mysk, myval)
            self.readers[k] = {}

    def op(self, eng, fn, reads=(), writes=()):
        waits = self._deps(eng, reads, writes)
        self.cnt[eng] += 1
        self.pending[eng].append((waits, fn, ('e', eng)))
        self._mark(('e', eng), self.cnt[eng], reads, writes)
        self.nins += 1

    def dma(self, out, in_, reads=(), writes=(), queue='sync', **kw):
        i = self.dnext % NDMA
        self.dnext += 1
        extra = [(('d', i), self.dcnt[i])] if self.dcnt[i] else None
        waits = self._deps(queue, reads, writes, extra)
        self.dcnt[i] += 16
        self.pending[queue].append((waits, lambda e: e.dma_start(out=out, in_=in_, **kw), ('d', i)))
        self._mark(('d', i), self.dcnt[i], reads, writes)
        self.nins += 1

    def barrier(self):
        for e in ENGS:
            extra = [(('e', o), self.cnt[o]) for o in ENGS if o != e and self.cnt[o]]
            extra += [(('d', i), self.dcnt[i]) for i in range(NDMA) if self.dcnt[i]]
            waits = self._deps(e, (), (), extra)
            if waits:
                self.pending[e].append((waits, None, None))

    def flush(self):
        nc = self.nc
        with nc.Block() as block:
            for e in ENGS:
                items = self.pending[e]
                if not items:
                    continue

                def body(eng, items=items):
                    for waits, fn, inc in items:
                        for sk, v in waits:
                            eng.wait_ge(self._sem(sk), v)
                        if fn is not None:
                            ins = fn(eng)
                            ins.then_inc(self._sem(inc), 1 if inc[0] == 'e' else 16)
                getattr(block, e)(body)
                self.pending[e] = []

    def end_phase(self):
        self.barrier()
        self.flush()

    def mm(self, out, lhsT, rhs, start, stop, r, w):
        self.op('tensor', lambda e: e.matmul(out, lhsT=lhsT, rhs=rhs, start=start, stop=stop), r, w)

    def tr(self, out, in_, ident, r, w):
        self.op('tensor', lambda e: e.transpose(out, in_, ident), r, w)

    def act(self, out, in_, func, r, w, bias=None, scale=None, accum=None):
        kw = {}
        if bias is not None:
            kw['bias'] = bias
        if scale is not None:
            kw['scale'] = scale
        if accum is not None:
            kw['accum_out'] = accum
        self.op('scalar', lambda e: e.activation(out=out, in_=in_, func=func, **kw), r, w)

    def tt(self, out, in0, in1, op, r, w, eng='vector'):
        self.op(eng, lambda e: e.tensor_tensor(out=out, in0=in0, in1=in1, op=op), r, w)

    def ts(self, out, in0, s1, s2, op0, op1, r, w, eng='vector'):
        if s2 is None:
            self.op(eng, lambda e: e.tensor_scalar(out=out, in0=in0, scalar1=s1, scalar2=None, op0=op0), r, w)
        else:
            self.op(eng, lambda e: e.tensor_scalar(out=out, in0=in0, scalar1=s1, scalar2=s2, op0=op0, op1=op1), r, w)

    def stt(self, out, in0, scalar, in1, op0, op1, r, w, eng='vector'):
        self.op(eng, lambda e: e.scalar_tensor_tensor(out=out, in0=in0, scalar=scalar, in1=in1, op0=op0, op1=op1), r, w)

    def cp(self, out, in_, r, w, eng='vector'):
        if eng == 'scalar':
            self.op(eng, lambda e: e.copy(out=out, in_=in_), r, w)
        else:
            self.op(eng, lambda e: e.tensor_copy(out=out, in_=in_), r, w)

    def red(self, out, in_, op, r, w, eng='vector'):
        self.op(eng, lambda e: e.tensor_reduce(out=out, in_=in_, axis=AX.X, op=op), r, w)

    def memset(self, ap, val, w, eng='vector'):
        self.op(eng, lambda e: e.memset(ap, val), (), w)


def tile_info(ti):
    lb = ti // TPB
    is_ctx = (ti % TPB) < 2
    who = 2 if is_ctx else lb
    return lb, is_ctx, who


def build(dbg=False, upto=99, layers=DEPTH):
    nc = bass.Bass("TRN2", target_bir_lowering=False)

    def din(name, shape, dt=F32):
        return nc.dram_tensor(name, list(shape), dt, kind="ExternalInput").ap()

    def dscr(name, shape, dt=F32):
        return nc.dram_tensor(name, list(shape), dt, kind="ExternalOutput" if dbg else "Internal").ap()

    xin = din("xin", [T, D])
    cT = din("cT", [128, 8, 3])
    ada_w = din("ada_w", [DEPTH, D, 6 * D])
    ada_b = din("ada_b", [DEPTH, 1, 6 * D])
    ada_bT = din("ada_bT", [DEPTH, 128, 48])
    n1g = din("n1g", [DEPTH, 128, 8])
    n2g = din("n2g", [DEPTH, 128, 8])
    fin_g = din("fin_g", [1, D])
    w_in = din("w_in", [DEPTH, D, D_IN])
    w_out = din("w_out", [DEPTH, D, D])
    rw_w_up = din("rw_w_up", [DEPTH, 128, 256])
    rw_a_up = din("rw_a_up", [DEPTH, 128, 256])
    rw_w0 = din("rw_w0", [DEPTH, 1, 512])
    rw_a0 = din("rw_a0", [DEPTH, 1, 512])
    rw_g_up = din("rw_g_up", [DEPTH, 128, 256])
    rw_k_k = din("rw_k_k", [DEPTH, 1, 256])
    rw_k_a = din("rw_k_a", [DEPTH, 1, 256])
    rw_r_k = din("rw_r_k", [DEPTH, 128, 4])
    rw_lnx = din("rw_lnx", [DEPTH, 128, 4])
    na_bias = din("na_bias", [DEPTH, 8, 128, 5, 640])
    sc_w = din("sc_w", [DEPTH, 128, 2, 3])
    pq_w = din("pq_w", [DEPTH, D, 2048])
    pkT = din("pkT", [DEPTH, 128, 16, 128])
    NE_D = NEXP if upto >= 8 else 256
    puT = din("puT", [DEPTH, D, NE_D])
    pv = din("pv", [DEPTH, NE_D, D])
    ident_d = din("ident", [128, 128])
    jblk_d = din("jblk", [128, 128])
    out_d = nc.dram_tensor("out", [LB * SEQ, D], F32, kind="ExternalOutput").ap()

    X = dscr("X", [T, D])
    MODd = dscr("MODd", [3, 6 * D])
    ZTf = dscr("ZTf", [9 * 128, T])
    ZTb = dscr("ZTb", [14 * 128, T], BF16)
    Ztok = dscr("Ztok", [T, 768])
    Vtok = dscr("Vtok", [T, 512], BF16)
    Pd = dscr("Pd", [LB, 2, L, 5, 256])
    YcatT = dscr("YcatT", [D, T], BF16)
    H2T = dscr("H2T", [D, T], BF16)
    Gd = dscr("Gd", [T, NE_D], BF16)

    with ExitStack() as top:
        P = Prog(nc, top)

        uid = [0]

        def sbt(st, name, shape, dt=F32):
            uid[0] += 1
            return st.enter_context(nc.sbuf_tensor("%s_s%d" % (name, uid[0]), list(shape), dt))

        def pst(st, name, shape, dt=F32):
            uid[0] += 1
            return st.enter_context(nc.psum_tensor("%s_p%d" % (name, uid[0]), list(shape), dt))

        ident = sbt(top, "ident", [128, 128])
        identb = sbt(top, "identb", [128, 128], BF16)
        jblk = sbt(top, "jblk", [128, 128])
        scT = sbt(top, "scT", [128, 8, 3])
        modT = sbt(top, "modT", [128, 48, 3])
        A1 = sbt(top, "A1", [128, 8, 3])
        A2 = sbt(top, "A2", [128, 8, 3])
        g1b = sbt(top, "g1b", [128, 3, D])
        g2b = sbt(top, "g2b", [128, 3, D])
        ones1 = sbt(top, "ones1", [1, 128])
        eps_n = sbt(top, "eps_n", [128, 1])

        P.dma(ident[:], ident_d, writes=['ident'])
        P.dma(jblk[:], jblk_d, writes=['jblk'])
        P.dma(scT[:], cT, writes=['scT'])
        P.cp(identb[:], ident[:], ['ident'], ['identb'])
        P.act(scT[:], scT[:], AF.Silu, ['scT'], ['scT'])
        P.memset(ones1[:], 1.0, ['ones1'])
        for i in range(4):
            P.dma(X[i * (T // 4):(i + 1) * (T // 4), :], xin[i * (T // 4):(i + 1) * (T // 4), :], writes=['X'])
        P.end_phase()

        def rmsnorm_tile(st_tiles, xt, key_x, xn, key_xn, small, key_small):
            P.act(xn[:], xt[:], AF.Square, [key_x], [key_xn, key_small], accum=small[:, 0:1])
            P.ts(small[:, 1:2], small[:, 0:1], 1.0 / D, NORM_EPS, ALU.mult, ALU.add, [key_small], [key_small])
            P.act(small[:, 2:3], small[:, 1:2], AF.Sqrt, [key_small], [key_small])
            P.op('vector', lambda e: e.reciprocal(out=small[:, 3:4], in_=small[:, 2:3]), [key_small], [key_small])
            P.ts(xn[:], xt[:], small[:, 3:4], None, ALU.mult, None, [key_x, key_small], [key_xn])

        for l in range(layers):
            last = (l == DEPTH - 1)
            tiles_all = list(range(NT))
            tiles_need = [ti for ti in tiles_all if not (last and tile_info(ti)[1])]

            with ExitStack() as st:
                wbuf = [sbt(st, "adaw%d" % i, [128, 8, 512]) for i in range(2)]
                bT = sbt(st, "adabT", [128, 48])
                brow = sbt(st, "adabrow", [3, 6 * D])
                mrow = sbt(st, "mrow", [3, 6 * D])
                gn1 = sbt(st, "gn1", [128, 8])
                gn2 = sbt(st, "gn2", [128, 8])
                psT = [pst(st, "psT%d" % i, [128, 512]) for i in range(2)]
                psR = [pst(st, "psR%d" % i, [128, 512]) for i in range(2)]
                P.dma(bT[:], ada_bT[l], writes=['bT'])
                P.dma(brow[:], ada_b[l].partition_broadcast(3)[:, 0, :], writes=['brow'])
                P.dma(gn1[:], n1g[l], writes=['gn1'])
                P.dma(gn2[:], n2g[l], writes=['gn2'])
                aw = ada_w[l].rearrange("(kc p) n -> p kc n", p=128)
                for nb in range(12):
                    wb = wbuf[nb % 2]
                    wk = 'adaw%d' % (nb % 2)
                    for kc in range(8):
                        P.dma(wb[:, kc, :], aw[:, kc, nb * 512:(nb + 1) * 512], writes=[wk])
                    pT = psT[nb % 2]
                    pR = psR[nb % 2]
                    kT_ = 'psT%d' % (nb % 2)
                    kR_ = 'psR%d' % (nb % 2)
                    for j in range(4):
                        for kc in range(8):
                            P.mm(pT[:, j * 3:(j + 1) * 3], wb[:, kc, j * 128:(j + 1) * 128], scT[:, kc, :],
                                 kc == 0, kc == 7, [wk, 'scT'], [kT_])
                    for kc in range(8):
                        P.mm(pR[0:3, :], scT[:, kc, :], wb[:, kc, :], kc == 0, kc == 7, [wk, 'scT'], [kR_])
                    for j in range(4):
                        jj = nb * 4 + j
                        P.ts(modT[:, jj, :], pT[:, j * 3:(j + 1) * 3], bT[:, jj:jj + 1], None, ALU.add, None,
                             [kT_, 'bT'], ['modT'])
                    P.tt(mrow[:, nb * 512:(nb + 1) * 512], pR[0:3, :], brow[:, nb * 512:(nb + 1) * 512], ALU.add,
                         [kR_, 'brow'], ['mrow'])
                P.dma(MODd, mrow[:], reads=['mrow'], writes=['MODd'])
                for who in range(3):
                    P.dma(g1b[:, who, :], MODd[who:who + 1, 2 * D:3 * D].partition_broadcast(128)[:, 0, :],
                          reads=['MODd'], writes=['g1b'])
                    P.dma(g2b[:, who, :], MODd[who:who + 1, 5 * D:6 * D].partition_broadcast(128)[:, 0, :],
                          reads=['MODd'], writes=['g2b'])
                for who in range(3):
                    P.stt(A1[:, :, who], modT[:, 8:16, who], 1.0, gn1[:], ALU.add, ALU.mult, ['modT', 'gn1'], ['A1'])
                    P.stt(A2[:, :, who], modT[:, 32:40, who], 1.0, gn2[:], ALU.add, ALU.mult, ['modT', 'gn2'], ['A2'])
                P.end_phase()
            if upto <= 0:
                break

            with ExitStack() as st:
                wsb = sbt(st, "w_in_sb", [128, 8, D_IN], BF16)
                for kc in range(8):
                    P.dma(wsb[:, kc, :], w_in[l][kc * 128:(kc + 1) * 128, :], writes=['wsb'], queue='gpsimd')
                xts = [sbt(st, "p1x%d" % i, [128, D]) for i in range(2)]
                xns = [sbt(st, "p1xn%d" % i, [128, D]) for i in range(2)]
                smalls = [sbt(st, "p1s%d" % i, [128, 4]) for i in range(2)]
                hTs = [sbt(st, "p1hT%d" % i, [128, 8, 512], BF16) for i in range(2)]
                stf = [sbt(st, "p1stf%d" % i, [128, 512]) for i in range(3)]
                stb = [sbt(st, "p1stb%d" % i, [128, 512], BF16) for i in range(3)]
                stt_ = [sbt(st, "p1stt%d" % i, [128, 768]) for i in range(2)]
                stv = [sbt(st, "p1stv%d" % i, [128, 512], BF16) for i in range(2)]
                pT = [pst(st, "p1pT%d" % i, [128, 512]) for i in range(2)]
                pM = [pst(st, "p1pM%d" % i, [128, 512]) for i in range(4)]
                fm_chunks = list(range(0, 17)) + list(range(21, 27))
                cnt_x = 0
                cnt_pT = 0
                cnt_pM = 0
                cnt_f = 0
                cnt_b = 0
                cnt_t = 0
                for g in range(NT // 4):
                    hT = hTs[g % 2]
                    kh = 'p1hT%d' % (g % 2)
                    for q in range(4):
                        ti = g * 4 + q
                        lb, is_ctx, who = tile_info(ti)
                        xi = cnt_x % 2
                        cnt_x += 1
                        xt, xn, sm = xts[xi], xns[xi], smalls[xi]
                        P.dma(xt[:], X[ti * 128:(ti + 1) * 128, :], writes=['p1x%d' % xi])
                        rmsnorm_tile(None, xt, 'p1x%d' % xi, xn, 'p1xn%d' % xi, sm, 'p1s%d' % xi)
                        for half in range(2):
                            pt = pT[cnt_pT % 2]
                            kp = 'p1pT%d' % (cnt_pT % 2)
                            cnt_pT += 1
                            for c4 in range(4):
                                c = half * 4 + c4
                                P.tr(pt[:, c4 * 128:(c4 + 1) * 128], xn[:, c * 128:(c + 1) * 128], ident[:],
                                     ['p1xn%d' % xi, 'ident'], [kp])
                            for c4 in range(4):
                                c = half * 4 + c4
                                P.ts(hT[:, c, q * 128:(q + 1) * 128], pt[:, c4 * 128:(c4 + 1) * 128],
                                     A1[:, c, who:who + 1], modT[:, c, who:who + 1], ALU.mult, ALU.add,
                                     [kp, 'A1', 'modT'], [kh])
                    for n in fm_chunks:
                        pm = pM[cnt_pM % 4]
                        kp = 'p1pM%d' % (cnt_pM % 4)
                        cnt_pM += 1
                        for kc in range(8):
                            P.mm(pm[:], wsb[:, kc, n * 128:(n + 1) * 128], hT[:, kc, :], kc == 0, kc == 7,
                                 ['wsb', kh], [kp])
                        if n < 9:
                            sf = stf[cnt_f % 3]
                            ks = 'p1stf%d' % (cnt_f % 3)
                            cnt_f += 1
                            P.cp(sf[:], pm[:], [kp], [ks], eng='scalar')
                            P.dma(ZTf[n * 128:(n + 1) * 128, g * 512:(g + 1) * 512], sf[:], reads=[ks])
                        else:
                            zi = n - 9 if n < 17 else n - 21 + 8
                            sf = stb[cnt_b % 3]
                            ks = 'p1stb%d' % (cnt_b % 3)
                            cnt_b += 1
                            P.cp(sf[:], pm[:], [kp], [ks], eng='scalar' if (cnt_b % 2) else 'vector')
                            P.dma(ZTb[zi * 128:(zi + 1) * 128, g * 512:(g + 1) * 512], sf[:], reads=[ks])
                    for q in range(4):
                        ti = g * 4 + q
                        si = cnt_t % 2
                        cnt_t += 1
                        for (c0, cw, dst, off) in ((0, 512, 'z', 0), (512, 256, 'z', 512), (2176, 512, 'v', 0)):
                            pm = pM[cnt_pM % 4]
                            kp = 'p1pM%d' % (cnt_pM % 4)
                            cnt_pM += 1
                            for kc in range(8):
                                P.mm(pm[:, 0:cw], hT[:, kc, q * 128:(q + 1) * 128], wsb[:, kc, c0:c0 + cw],
                                     kc == 0, kc == 7, ['wsb', kh], [kp])
                            if dst == 'z':
                                P.cp(stt_[si][:, off:off + cw], pm[:, 0:cw], [kp], ['p1stt%d' % si], eng='vector')
                            else:
                                P.cp(stv[si][:], pm[:, 0:cw], [kp], ['p1stv%d' % si], eng='scalar')
                        P.dma(Ztok[ti * 128:(ti + 1) * 128, :], stt_[si][:], reads=['p1stt%d' % si])
                        P.dma(Vtok[ti * 128:(ti + 1) * 128, :], stv[si][:], reads=['p1stv%d' % si])
                P.end_phase()
            if upto <= 1:
                break

            with ExitStack() as st:
                wup = sbt(st, "wup", [128, 256])
                aup = sbt(st, "aup", [128, 256])
                w0r = sbt(st, "w0r", [1, 512])
                a0r = sbt(st, "a0r", [1, 512])
                kkb = sbt(st, "kkb", [128, 256])
                kab = sbt(st, "kab", [128, 256])
                P.dma(wup[:], rw_w_up[l], writes=['wup'])
                P.dma(aup[:], rw_a_up[l], writes=['aup'])
                P.dma(w0r[:], rw_w0[l], writes=['w0r'])
                P.dma(a0r[:], rw_a0[l], writes=['a0r'])
                P.dma(kkb[:], rw_k_k[l].partition_broadcast(128)[:, 0, :], writes=['kkb'])
                P.dma(kab[:], rw_k_a[l].partition_broadcast(128)[:, 0, :], writes=['kab'])
                NB2 = 2
                zt = [sbt(st, "p2z%d" % i, [128, 768]) for i in range(NB2)]
                lwa = [sbt(st, "p2lw%d" % i, [128, 2, 128]) for i in range(NB2)]
                pr = [sbt(st, "p2pr%d" % i, [128, 5, 2, 256]) for i in range(NB2)]
                asb = [sbt(st, "p2a%d" % i, [128, 2, 256]) for i in range(NB2)]
                tmp = [sbt(st, "p2t%d" % i, [128, 2, 256]) for i in range(NB2)]
                sm = [sbt(st, "p2s%d" % i, [128, 16]) for i in range(NB2)]
                psw = [pst(st, "p2pw%d" % i, [128, 512]) for i in range(2)]
                psa = [pst(st, "p2pa%d" % i, [128, 512]) for i in range(2)]
                for ti in range(NT):
                    b = ti % NB2
                    lb = ti // TPB
                    tl = (ti % TPB) * 128
                    z, lw, prow, a_, t_, s_ = zt[b], lwa[b], pr[b], asb[b], tmp[b], sm[b]
                    kz, klw, kpr, ka, kt, ks = ('p2z%d' % b, 'p2lw%d' % b, 'p2pr%d' % b, 'p2a%d' % b, 'p2t%d' % b, 'p2s%d' % b)
                    pw, pa = psw[b], psa[b]
                    kpw, kpa = 'p2pw%d' % b, 'p2pa%d' % b
                    P.dma(z[:], Ztok[ti * 128:(ti + 1) * 128, :], writes=[kz])
                    P.dma(lw[:, 0, :], ZTf[6 * 128:7 * 128, ti * 128:(ti + 1) * 128], writes=[klw])
                    P.dma(lw[:, 1, :], ZTf[7 * 128:8 * 128, ti * 128:(ti + 1) * 128], writes=[klw])
                    P.act(lw[:, 0, :], lw[:, 0, :], AF.Tanh, [klw], [klw])
                    for d in range(2):
                        P.mm(pw[:, d * 256:(d + 1) * 256], lw[d * 64:(d + 1) * 64, 0, :], wup[d * 64:(d + 1) * 64, :],
                             True, False, [klw, 'wup'], [kpw])
                        P.mm(pw[:, d * 256:(d + 1) * 256], ones1[0:1, :], w0r[0:1, d * 256:(d + 1) * 256],
                             False, True, ['ones1', 'w0r'], [kpw])
                        P.mm(pa[:, d * 256:(d + 1) * 256], lw[d * 64:(d + 1) * 64, 1, :], aup[d * 64:(d + 1) * 64, :],
                             True, False, [klw, 'aup'], [kpa])
                        P.mm(pa[:, d * 256:(d + 1) * 256], ones1[0:1, :], a0r[0:1, d * 256:(d + 1) * 256],
                             False, True, ['ones1', 'a0r'], [kpa])
                    P.act(t_[:].rearrange("p a b -> p (a b)"), pw[:], AF.Sigmoid, [kpw], [kt])
                    P.act(prow[:, 1, :, :], t_[:], AF.Exp, [kt], [kpr], scale=-float(np.exp(-0.5)))
                    P.act(a_[:].rearrange("p a b -> p (a b)"), pa[:], AF.Sigmoid, [kpa], [ka])
                    r_ = z[:, 0:256]
                    k_ = z[:, 256:512]
                    P.tt(prow[:, 0, 0, :], k_, kkb[:], ALU.mult, [kz, 'kkb'], [kpr])
                    P.tt(t_[:, 0, :], prow[:, 0, 0, :], prow[:, 0, 0, :], ALU.mult, [kpr], [kt])
                    P.red(s_[:, 0:4], t_[:, 0, :].rearrange("p (h j) -> p h j", h=4), ALU.add, [kt], [ks])
                    P.ts(s_[:, 4:8], s_[:, 0:4], 1e-12, None, ALU.add, None, [ks], [ks])
                    P.act(s_[:, 8:12], s_[:, 4:8], AF.Sqrt, [ks], [ks])
                    P.op('vector', lambda e, s_=s_: e.reciprocal(out=s_[:, 12:16], in_=s_[:, 8:12]), [ks], [ks])
                    P.tt(prow[:, 0, 0, :].rearrange("p (h j) -> p h j", h=4),
                         prow[:, 0, 0, :].rearrange("p (h j) -> p h j", h=4),
                         s_[:, 12:16].unsqueeze(2).broadcast_to([128, 4, 64]), ALU.mult, [kpr, ks], [kpr])
                    P.cp(prow[:, 0, 1, :], prow[:, 0, 0, :], [kpr], [kpr], eng='gpsimd')
                    P.cp(prow[:, 4, 0, :], r_, [kz], [kpr], eng='gpsimd')
                    P.cp(prow[:, 4, 1, :], r_, [kz], [kpr], eng='gpsimd')
                    P.tt(prow[:, 2, :, :], a_[:], prow[:, 0, :, :], ALU.mult, [ka, kpr], [kpr])
                    P.stt(t_[:], a_[:], -1.0, kab[:].unsqueeze(1).broadcast_to([128, 2, 256]), ALU.add, ALU.mult,
                          [ka, 'kab'], [kt])
                    P.stt(prow[:, 3, :, :], t_[:], 1.0, k_.unsqueeze(1).broadcast_to([128, 2, 256]), ALU.add, ALU.mult,
                          [kt, kz], [kpr])
                    for d in range(2):
                        P.dma(Pd[lb, d, tl:tl + 128], prow[:, :, d, :], reads=[kpr])
                P.end_phase()
            if upto <= 2:
                break

            with ExitStack() as st:
                Y = sbt(st, "scanY", [128, 2, 4, L])
                with ExitStack() as st3:
                    Vs = sbt(st3, "scanV", [128, 4, L])
                    S = sbt(st3, "scanS", [128, 2, 256])
                    NST = 2
                    bcs = [sbt(st3, "scanB%d" % i, [128, NST, 2, 5, 256]) for i in range(2)]
                    t1 = sbt(st3, "scanT1", [128, 2, 256])
                    t2 = [sbt(st3, "scanT2%d" % i, [128, 2, 256]) for i in range(2)]
                    sa = sbt(st3, "scanSa", [128, 8])
                    for lb in range(LB):
                        src = bass.AP(ZTf.tensor, 512 * T + lb * L, [[T, 64], [64 * T, 4], [1, L]])
                        P.dma(Vs[lb * 64:(lb + 1) * 64, :, :], src, writes=['scanV'])
                    P.memset(S[:], 0.0, ['scanS'])
                    S3 = S[:].rearrange("p d (h j) -> p (d h) j", h=4)
                    ngroups = L // NST
                    for gi in range(ngroups):
                        bc = bcs[gi % 2]
                        kb = 'scanB%d' % (gi % 2)
                        s0 = gi * NST
                        for lb in range(LB):
                            for d in range(2):
                                if d == 0:
                                    tok0, stp = s0, 1
                                else:
                                    tok0 = (CTX - 1 - s0) if s0 < CTX else (L + CTX - 1 - s0)
                                    stp = -1
                                src = bass.AP(Pd.tensor, ((lb * 2 + d) * L + tok0) * 1280,
                                              [[0, 64], [stp * 1280, NST], [1, 1280]])
                                P.dma(bc[lb * 64:(lb + 1) * 64, :, d, :, :].rearrange("p k v c -> p k (v c)"), src, writes=[kb])
                        for k in range(NST):
                            s = s0 + k
                            tk0 = s
                            tk1 = (CTX - 1 - s) if s < CTX else (L + CTX - 1 - s)
                            kkv, wv, bv, kv, rv = (bc[:, k, :, i, :] for i in range(5))
                            tb = t2[s % 2]
                            ktb = 'scanT2%d' % (s % 2)
                            vap = bass.AP(Vs, Vs[:].offset + tk0, [[4 * L, 128], [tk1 - tk0, 2], [L, 4], [0, 64]])
                            P.tt(tb[:].rearrange("p d (h j) -> p d h j", h=4), vap,
                                 kv.rearrange("p d (h j) -> p d h j", h=4), ALU.mult, ['scanV', kb], [ktb], eng='gpsimd')
                            P.tt(t1[:], S[:], kkv, ALU.mult, ['scanS', kb], ['scanT1'])
                            P.red(sa[:], t1[:].rearrange("p d (h j) -> p (d h) j", h=4), ALU.add, ['scanT1'], ['scanSa'])
                            P.tt(S[:], S[:], wv, ALU.mult, ['scanS', kb], ['scanS'])
                            P.tt(t1[:].rearrange("p d (h j) -> p d h j", h=4),
                                 bv.rearrange("p d (h j) -> p d h j", h=4),
                                 sa[:].rearrange("p (d h) -> p d h", d=2).unsqueeze(3).broadcast_to([128, 2, 4, 64]), ALU.mult, ['scanSa', kb], ['scanT1'])
                            P.tt(S[:], S[:], t1[:], ALU.subtract, ['scanS', 'scanT1'], ['scanS'])
                            P.tt(S[:], S[:], tb[:], ALU.add, ['scanS', ktb], ['scanS'])
                            P.tt(t1[:], S[:], rv, ALU.mult, ['scanS', kb], ['scanT1'])
                            P.red(Y[:, :, :, s], t1[:].rearrange("p d (h j) -> p d h j", h=4), ALU.add, ['scanT1'], ['scanY'])
                    P.end_phase()
                if upto <= 3:
                    break
                P.tt(Y[:, 0, :, 0:CTX], Y[:, 0, :, 0:CTX], Y[:, 1, :, CTX - 1::-1] if False else
                     bass.AP(Y, Y[:].offset + 4 * L + CTX - 1, [[8 * L, 128], [L, 4], [-1, CTX]]),
                     ALU.add, ['scanY'], ['scanY'])
                P.tt(Y[:, 0, :, CTX:L], Y[:, 0, :, CTX:L],
                     bass.AP(Y, Y[:].offset + 4 * L + L - 1, [[8 * L, 128], [L, 4], [-1, SEQ]]),
                     ALU.add, ['scanY'], ['scanY'])
                with ExitStack() as st4:
                    BW = 384
                    gupA = sbt(st4, "gupA", [128, 4, 128])
                    gupB = sbt(st4, "gupB", [128, 4, 128])
                    rk_ = sbt(st4, "rwrk", [128, 4])
                    lnx = sbt(st4, "rwlnx", [128, 4])
                    P.memset(gupA[:], 0.0, ['gupA'])
                    P.memset(gupB[:], 0.0, ['gupB'])
                    gsrc = rw_g_up[l].rearrange("r (h i) -> r h i", h=4)
                    P.dma(gupA[:, :, 0:64], gsrc, writes=['gupA'])
                    P.dma(gupB[:, :, 64:128], gsrc, writes=['gupB'])
                    P.dma(rk_[:], rw_r_k[l], writes=['rwrk'])
                    P.dma(lnx[:], rw_lnx[l], writes=['rwlnx'])
                    NB4 = 2
                    rb = [sbt(st4, "p4r%d" % i, [128, BW]) for i in range(NB4)]
                    kb_ = [sbt(st4, "p4k%d" % i, [128, BW]) for i in range(NB4)]
                    vb = [sbt(st4, "p4v%d" % i, [128, BW]) for i in range(NB4)]
                    sg = [sbt(st4, "p4sg%d" % i, [128, 2, BW]) for i in range(NB4)]
                    yc = [sbt(st4, "p4yc%d" % i, [128, BW]) for i in range(NB4)]
                    sq = [sbt(st4, "p4sq%d" % i, [128, BW]) for i in range(NB4)]
                    yo = [sbt(st4, "p4yo%d" % i, [128, BW], BF16) for i in range(NB4)]
                    pm_ = [pst(st4, "p4pm%d" % i, [128, 512]) for i in range(2)]
                    pv_ = [pst(st4, "p4pv%d" % i, [128, 512]) for i in range(2)]
                    pb_ = [pst(st4, "p4pb%d" % i, [128, 512]) for i in range(2)]
                    pg_ = [pst(st4, "p4pg%d" % i, [128, 512]) for i in range(2)]
                    it = 0
                    for h in range(4):
                        for blk in range(L // BW):
                            b = it % NB4
                            it += 1
                            c0 = blk * BW
                            kr, kk_, kv_, ksg, kyc, ksq, kyo = ('p4r%d' % b, 'p4k%d' % b, 'p4v%d' % b, 'p4sg%d' % b,
                                                               'p4yc%d' % b, 'p4sq%d' % b, 'p4yo%d' % b)
                            kpm, kpv, kpb, kpg = 'p4pm%d' % b, 'p4pv%d' % b, 'p4pb%d' % b, 'p4pg%d' % b
                            for lb in range(LB):
                                for (dst, kd, row0) in ((rb[b], kr, 0), (kb_[b], kk_, 256), (vb[b], kv_, 512)):
                                    P.dma(dst[lb * 64:(lb + 1) * 64, :],
                                          ZTf[row0 + h * 64:row0 + (h + 1) * 64, lb * L + c0:lb * L + c0 + BW], writes=[kd])
                                P.dma(sg[b][:, lb, :], ZTf[8 * 128:9 * 128, lb * L + c0:lb * L + c0 + BW], writes=[ksg])
                            ys = Y[:, 0, h, c0:c0 + BW]
                            P.mm(pm_[b][:, 0:BW], jblk[:], ys, True, True, ['jblk', 'scanY'], [kpm])
                            P.stt(yc[b][:], pm_[b][:, 0:BW], -1.0 / 64, ys, ALU.mult, ALU.add, [kpm, 'scanY'], [kyc])
                            P.act(sq[b][:], yc[b][:], AF.Square, [kyc], [ksq])
                            P.mm(pv_[b][:, 0:BW], jblk[:], sq[b][:], True, True, ['jblk', ksq], [kpv])
                            P.ts(sq[b][:], pv_[b][:, 0:BW], 1.0 / 64, GN_EPS, ALU.mult, ALU.add, [kpv], [ksq])
                            P.act(sq[b][:], sq[b][:], AF.Sqrt, [ksq], [ksq])
                            P.op('vector', lambda e, o=sq[b]: e.reciprocal(out=o[:], in_=o[:]), [ksq], [ksq])
                            P.stt(yc[b][:], yc[b][:], lnx[:, h:h + 1], sq[b][:], ALU.mult, ALU.mult, [kyc, ksq, 'rwlnx'], [kyc])
                            P.stt(rb[b][:], rb[b][:], rk_[:, h:h + 1], kb_[b][:], ALU.mult, ALU.mult, [kr, kk_, 'rwrk'], [kr])
                            P.mm(pb_[b][:, 0:BW], jblk[:], rb[b][:], True, True, ['jblk', kr], [kpb])
                            P.tt(vb[b][:], pb_[b][:, 0:BW], vb[b][:], ALU.mult, [kpb, kv_], [kv_])
                            P.tt(yc[b][:], yc[b][:], vb[b][:], ALU.add, [kyc, kv_], [kyc], eng='gpsimd')
                            P.act(sg[b][:], sg[b][:], AF.Sigmoid, [ksg], [ksg])
                            P.mm(pg_[b][:, 0:BW], gupA[:, h, :], sg[b][:, 0, :], True, False, ['gupA', ksg], [kpg])
                            P.mm(pg_[b][:, 0:BW], gupB[:, h, :], sg[b][:, 1, :], False, True, ['gupB', ksg], [kpg])
                            P.tt(yo[b][:], pg_[b][:, 0:BW], yc[b][:], ALU.mult, [kpg, kyc], [kyo])
                            for lb in range(LB):
                                P.dma(YcatT[h * 64:(h + 1) * 64, lb * L + c0:lb * L + c0 + BW],
                                      yo[b][lb * 64:(lb + 1) * 64, :], reads=[kyo])
                    P.end_phase()
            if upto <= 4:
                break

            with ExitStack() as st:
                qT = sbt(st, "naq", [128, 4, L], BF16)
                kT = sbt(st, "nak", [128, 4, L], BF16)
                Vt = sbt(st, "nav", [128, TPB, 512], BF16)
                ya = sbt(st, "naya", [128, TPB, 512], BF16)
                bias = [sbt(st, "nab%d" % i, [128, 5, 640]) for i in range(2)]
                NB5 = 2
                s_sb = [sbt(st, "nas%d" % i, [128, 896]) for i in range(NB5)]
                p_sb = [sbt(st, "nap%d" % i, [128, 896], BF16) for i in range(NB5)]
                pT_sb = [sbt(st, "napT%d" % i, [128, 7, 128], BF16) for i in range(NB5)]
                sm5 = [sbt(st, "nasm%d" % i, [128, 4]) for i in range(NB5)]
                yT_sb = [sbt(st, "nayT%d" % i, [128, 4, 128], BF16) for i in range(2)]
                psA = [pst(st, "napsA%d" % i, [128, 512]) for i in range(2)]
                psB = [pst(st, "napsB%d" % i, [128, 512]) for i in range(2)]
                psT5 = [pst(st, "napsT%d" % i, [128, 1024], BF16) for i in range(2)]
                psO = [pst(st, "napsO%d" % i, [128, 512]) for i in range(2)]
                it = 0
                ito = 0
                for lb in range(LB):
                    for c in range(4):
                        P.dma(qT[:, c, :], ZTb[c * 128:(c + 1) * 128, lb * L:(lb + 1) * L], writes=['naq'])
                        P.dma(kT[:, c, :], ZTb[(4 + c) * 128:(5 + c) * 128, lb * L:(lb + 1) * L], writes=['nak'])
                    P.dma(Vt[:], Vtok[lb * L:(lb + 1) * L, :].rearrange("(t p) c -> p t c", p=128), writes=['nav'])
                    for h in range(8):
                        bs = bias[h % 2]
                        kbs = 'nab%d' % (h % 2)
                        P.dma(bs[:], na_bias[l, h], writes=[kbs])
                        hp = (h % 2) * 64
                        hc = h // 2
                        for qt in range(TPB):
                            if qt < 2 and last:
                                continue
                            b = it % NB5
                            it += 1
                            ks, kp, kpT, ksm = 'nas%d' % b, 'nap%d' % b, 'napT%d' % b, 'nasm%d' % b
                            kA, kB, kT5, kO = 'napsA%d' % b, 'napsB%d' % b, 'napsT%d' % b, 'napsO%d' % b
                            qh = qT[hp:hp + 64, hc, qt * 128:(qt + 1) * 128]
                            if qt >= 2:
                                r0 = 2 * (qt - 2)
                                s0_ = min(max(r0 - 4, 0), 24)
                                u0 = min(s0_, 22)
                                cfg = {0: 0, 2: 1, 28: 3, 30: 4}.get(r0, 2)
                                wt0 = CTX + u0 * 64
                                nk = 896
                                P.mm(psA[b][:, 0:512], qh, kT[hp:hp + 64, hc, wt0:wt0 + 512], True, True, ['naq', 'nak'], [kA])
                                P.mm(psB[b][:, 0:128], qh, kT[hp:hp + 64, hc, wt0 + 512:wt0 + 640], True, True, ['naq', 'nak'], [kB])
                                P.mm(psB[b][:, 128:384], qh, kT[hp:hp + 64, hc, 0:CTX], True, True, ['naq', 'nak'], [kB])
                                P.stt(s_sb[b][:, 0:512], psA[b][:, 0:512], 0.125, bs[:, cfg, 0:512], ALU.mult, ALU.add,
                                      [kA, kbs], [ks])
                                P.stt(s_sb[b][:, 512:640], psB[b][:, 0:128], 0.125, bs[:, cfg, 512:640], ALU.mult, ALU.add,
                                      [kB, kbs], [ks])
                                P.ts(s_sb[b][:, 640:896], psB[b][:, 128:384], 0.125, None, ALU.mult, None, [kB], [ks])
                                vtiles = [2 + u0 // 2 + j for j in range(5)] + [0, 1]
                            else:
                                nk = 256
                                P.mm(psB[b][:, 128:384], qh, kT[hp:hp + 64, hc, 0:CTX], True, True, ['naq', 'nak'], [kB])
                                P.ts(s_sb[b][:, 0:256], psB[b][:, 128:384], 0.125, None, ALU.mult, None, [kB], [ks])
                                vtiles = [0, 1]
                            sm = sm5[b]
                            P.op('vector', lambda e, o=sm, i_=s_sb[b], nk=nk: e.tensor_reduce(out=o[:, 0:1], in_=i_[:, 0:nk], axis=AX.X, op=ALU.max),
                                 [ks], [ksm])
                            P.ts(sm[:, 1:2], sm[:, 0:1], -1.0, None, ALU.mult, None, [ksm], [ksm])
                            P.act(p_sb[b][:, 0:nk], s_sb[b][:, 0:nk], AF.Exp, [ks, ksm], [kp, ksm], bias=sm[:, 1:2], accum=sm[:, 2:3])
                            P.op('vector', lambda e, o=sm: e.reciprocal(out=o[:, 3:4], in_=o[:, 2:3]), [ksm], [ksm])
                            nblk = nk // 128
                            for j in range(nblk):
                                P.tr(psT5[b][:, j * 128:(j + 1) * 128], p_sb[b][:, j * 128:(j + 1) * 128], identb[:],
                                     [kp, 'identb'], [kT5])
                            P.cp(pT_sb[b][:, 0:nblk, :].rearrange("p a b -> p (a b)"), psT5[b][:, 0:nk], [kT5], [kpT],
                                 eng='scalar')
                            for j in range(nblk):
                                P.mm(psO[b][:, 0:64], pT_sb[b][:, j, :], Vt[:, vtiles[j], h * 64:(h + 1) * 64],
                                     j == 0, j == nblk - 1, [kpT, 'nav'], [kO])
                            P.ts(ya[:, qt, h * 64:(h + 1) * 64], psO[b][:, 0:64], sm[:, 3:4], None, ALU.mult, None,
                                 [kO, ksm], ['naya'])
                    for qt in range(TPB):
                        if qt < 2 and last:
                            continue
                        b = ito % 2
                        ito += 1
                        kT5, kyT = 'napsT%d' % b, 'nayT%d' % b
                        for c in range(4):
                            P.tr(psT5[b][:, c * 128:(c + 1) * 128], ya[:, qt, c * 128:(c + 1) * 128], identb[:],
                                 ['naya', 'identb'], [kT5])
                        P.cp(yT_sb[b][:].rearrange("p a b -> p (a b)"), psT5[b][:, 0:512], [kT5], [kyT])
                        col = lb * L + qt * 128
                        P.dma(YcatT[256:768, col:col + 128].rearrange("(c p) t -> p c t", p=128), yT_sb[b][:], reads=[kyT])
                P.end_phase()
            if upto <= 5:
                break

            with ExitStack() as st:
                scw = sbt(st, "scw", [128, 2, 3])
                P.dma(scw[:], sc_w[l], writes=['scw'])
                HT_ = T // 2
                for cc in range(2):
                    for hf in range(LB):
                        bb = sbt(st, "scb%d%d" % (cc, hf), [128, L], BF16)
                        cb = sbt(st, "scc%d%d" % (cc, hf), [128, L], BF16)
                        xb = sbt(st, "scx%d%d" % (cc, hf), [128, L], BF16)
                        u = sbt(st, "scu%d%d" % (cc, hf), [128, L])
                        o = sbt(st, "sco%d%d" % (cc, hf), [128, L])
                        ob = sbt(st, "scob%d%d" % (cc, hf), [128, L], BF16)
                        kk = "%d%d" % (cc, hf)
                        cs = slice(hf * L, (hf + 1) * L)
                        P.dma(bb[:], ZTb[(8 + cc) * 128:(9 + cc) * 128, cs], writes=['scb' + kk])
                        P.dma(cb[:], ZTb[(10 + cc) * 128:(11 + cc) * 128, cs], writes=['scc' + kk])
                        P.dma(xb[:], ZTb[(12 + cc) * 128:(13 + cc) * 128, cs], writes=['scx' + kk])
                        P.tt(u[:], cb[:], xb[:], ALU.mult, ['scc' + kk, 'scx' + kk], ['scu' + kk])
                        P.ts(o[:], u[:], scw[:, cc, 1:2], None, ALU.mult, None, ['scu' + kk, 'scw'], ['sco' + kk])
                        for (a, e_) in ((0, CTX), (CTX, L)):
                            P.stt(o[:, a + 1:e_], u[:, a:e_ - 1], scw[:, cc, 0:1], o[:, a + 1:e_], ALU.mult, ALU.add,
                                  ['scu' + kk, 'scw', 'sco' + kk], ['sco' + kk])
                            P.stt(o[:, a:e_ - 1], u[:, a + 1:e_], scw[:, cc, 2:3], o[:, a:e_ - 1], ALU.mult, ALU.add,
                                  ['scu' + kk, 'scw', 'sco' + kk], ['sco' + kk])
                        P.tt(ob[:], o[:], bb[:], ALU.mult, ['sco' + kk, 'scb' + kk], ['scob' + kk])
                        P.dma(YcatT[(6 + cc) * 128:(7 + cc) * 128, cs], ob[:], reads=['scob' + kk])
                P.end_phase()
            if upto <= 6:
                break

            with ExitStack() as st:
                wo = sbt(st, "wo_sb", [128, 8, D], BF16)
                for kc in range(8):
                    P.dma(wo[:, kc, :], w_out[l][kc * 128:(kc + 1) * 128, :], writes=['wo'], queue='gpsimd')
                ycs = [sbt(st, "p7yc%d" % i, [128, 8, 128], BF16) for i in range(2)]
                xts = [sbt(st, "p7x%d" % i, [128, D]) for i in range(2)]
                xo = [sbt(st, "p7xo%d" % i, [128, D]) for i in range(2)]
                pso = [pst(st, "p7ps%d" % i, [128, 512]) for i in range(4)]
                for it, ti in enumerate(tiles_need):
                    b = it % 2
                    lb, is_ctx, who = tile_info(ti)
                    P.dma(ycs[b][:], YcatT[:, ti * 128:(ti + 1) * 128].rearrange("(c p) t -> p c t", p=128), writes=['p7yc%d' % b])
                    P.dma(xts[b][:], X[ti * 128:(ti + 1) * 128, :], writes=['p7x%d' % b])
                    for half in range(2):
                        pp = pso[(it * 2 + half) % 4]
                        kp = 'p7ps%d' % ((it * 2 + half) % 4)
                        for kc in range(8):
                            P.mm(pp[:], ycs[b][:, kc, :], wo[:, kc, half * 512:(half + 1) * 512], kc == 0, kc == 7,
                                 ['p7yc%d' % b, 'wo'], [kp])
                        P.tt(xo[b][:, half * 512:(half + 1) * 512], pp[:], g1b[:, who, half * 512:(half + 1) * 512], ALU.mult,
                             [kp, 'g1b'], ['p7xo%d' % b])
                    P.tt(xo[b][:], xo[b][:], xts[b][:], ALU.add, ['p7xo%d' % b, 'p7x%d' % b], ['p7xo%d' % b], eng='gpsimd')
                    P.dma(X[ti * 128:(ti + 1) * 128, :], xo[b][:], reads=['p7xo%d' % b])
                P.end_phase()
            if upto <= 7:
                break

            with ExitStack() as st:
                qw = sbt(st, "qw_sb", [128, 8, 2048], BF16)
                for kc in range(8):
                    P.dma(qw[:, kc, :], pq_w[l][kc * 128:(kc + 1) * 128, :], writes=['qw'], queue='gpsimd')
                keys = sbt(st, "pk_sb", [128, 16, 128], BF16)
                P.dma(keys[:], pkT[l], writes=['pk'], queue='gpsimd')
                xts = [sbt(st, "p8x%d" % i, [128, D]) for i in range(2)]
                xns = [sbt(st, "p8xn%d" % i, [128, D]) for i in range(2)]
                sms = [sbt(st, "p8s%d" % i, [128, 4]) for i in range(2)]
                h2 = [sbt(st, "p8h%d" % i, [128, 8, 128], BF16) for i in range(2)]
                qTs = [sbt(st, "p8q%d" % i, [128, 16, 128], BF16) for i in range(2)]
                ssb = [sbt(st, "p8sc%d" % i, [128, 2048]) for i in range(2)]
                Gs = [sbt(st, "p8G%d" % i, [128, NEXP], BF16) for i in range(1)]
                wk = sbt(st, "p8wk", [128, 256])
                t16 = sbt(st, "p8t16", [128, 2, 16])
                cand = sbt(st, "p8cand", [128, 256])
                c16 = sbt(st, "p8c16", [128, 16])
                e16 = sbt(st, "p8e16", [128, 16])
                hs = sbt(st, "p8hs", [128, 8])
                IB = 16
                Db = [sbt(st, "p8D%d" % i, [128, IB, 128]) for i in range(2)]
                Eb = [sbt(st, "p8E%d" % i, [128, IB, 128]) for i in range(2)]
                Tb = [sbt(st, "p8T%d" % i, [128, IB, 128], BF16) for i in range(2)]
                pT = [pst(st, "p8pT%d" % i, [128, 512]) for i in range(2)]
                pQ = [pst(st, "p8pQ%d" % i, [128, 512]) for i in range(2)]
                pS = [pst(st, "p8pS%d" % i, [128, 512]) for i in range(4)]
                cnt_pT = 0
                cnt_pQ = 0
                cnt_D = 0
                for it, ti in enumerate(tiles_need):
                    b = it % 2
                    lb, is_ctx, who = tile_info(ti)
                    xt, xn, sm = xts[b], xns[b], sms[b]
                    kx, kxn, ksm, kh, kq, ksc, kG = ('p8x%d' % b, 'p8xn%d' % b, 'p8s%d' % b, 'p8h%d' % b, 'p8q%d' % b,
                                                     'p8sc%d' % b, 'p8G%d' % b)
                    P.dma(xt[:], X[ti * 128:(ti + 1) * 128, :], writes=[kx])
                    rmsnorm_tile(None, xt, kx, xn, kxn, sm, ksm)
                    for half in range(2):
                        pt = pT[cnt_pT % 2]
                        kp = 'p8pT%d' % (cnt_pT % 2)
                        cnt_pT += 1
                        for c4 in range(4):
                            c = half * 4 + c4
                            P.tr(pt[:, c4 * 128:(c4 + 1) * 128], xn[:, c * 128:(c + 1) * 128], ident[:], [kxn, 'ident'], [kp])
                        for c4 in range(4):
                            c = half * 4 + c4
                            P.ts(h2[b][:, c, :], pt[:, c4 * 128:(c4 + 1) * 128], A2[:, c, who:who + 1],
                                 modT[:, 24 + c, who:who + 1], ALU.mult, ALU.add, [kp, 'A2', 'modT'], [kh])
                    P.dma(H2T[:, ti * 128:(ti + 1) * 128].rearrange("(c p) t -> p c t", p=128), h2[b][:], reads=[kh])
                    for c4g in range(4):
                        pq = pQ[cnt_pQ % 2]
                        kp = 'p8pQ%d' % (cnt_pQ % 2)
                        cnt_pQ += 1
                        for c4 in range(4):
                            c = c4g * 4 + c4
                            for kc in range(8):
                                P.mm(pq[:, c4 * 128:(c4 + 1) * 128], qw[:, kc, c * 128:(c + 1) * 128], h2[b][:, kc, :],
                                     kc == 0, kc == 7, ['qw', kh], [kp])
                        P.cp(qTs[b][:, c4g * 4:(c4g + 1) * 4, :].rearrange("p a b -> p (a b)"), pq[:], [kp], [kq], eng='scalar')
                    for c4g in range(4):
                        pp = pS[c4g]
                        kp = 'p8pS%d' % c4g
                        for c4 in range(4):
                            c = c4g * 4 + c4
                            P.mm(pp[:, c4 * 128:(c4 + 1) * 128], qTs[b][:, c, :], keys[:, c, :], True, True, [kq, 'pk'], [kp])
                        P.cp(ssb[b][:, c4g * 512:(c4g + 1) * 512], pp[:], [kp], [ksc], eng='scalar')
                    G = Gs[0]
                    kG = 'p8G0'
                    P.memset(G[:], 0.0, [kG], eng='gpsimd')
                    for h in range(8):
                        s1 = ssb[b][:, h * 256:h * 256 + 128]
                        s2 = ssb[b][:, h * 256 + 128:h * 256 + 256]
                        for p_, sx in ((0, s1), (1, s2)):
                            P.op('vector', lambda e, sx=sx, p_=p_: e.max(out=t16[:, p_, 0:8], in_=sx), [ksc], ['p8t16'])
                            P.op('vector', lambda e, sx=sx, p_=p_: e.match_replace(out=wk[:, 0:128], in_to_replace=t16[:, p_, 0:8],
                                                                              in_values=sx, imm_value=-1e30), [ksc, 'p8t16'], ['p8wk'])
                            P.op('vector', lambda e, p_=p_: e.max(out=t16[:, p_, 8:16], in_=wk[:, 0:128]), ['p8wk'], ['p8t16'])
                        P.tt(cand[:].rearrange("p (a b) -> p a b", a=16), t16[:, 0, :].unsqueeze(2).broadcast_to([128, 16, 16]),
                             t16[:, 1, :].unsqueeze(1).broadcast_to([128, 16, 16]), ALU.add, ['p8t16'], ['p8cand'])
                        P.op('vector', lambda e: e.max(out=c16[:, 0:8], in_=cand[:]), ['p8cand'], ['p8c16'])
                        P.op('vector', lambda e: e.match_replace(out=wk[:], in_to_replace=c16[:, 0:8], in_values=cand[:], imm_value=-1e30),
                             ['p8cand', 'p8c16'], ['p8wk'])
                        P.op('vector', lambda e: e.max(out=c16[:, 8:16], in_=wk[:]), ['p8wk'], ['p8c16'])
                        P.ts(hs[:, 0:1], c16[:, 0:1], -1.0, None, ALU.mult, None, ['p8c16'], ['p8hs'])
                        P.act(e16[:], c16[:], AF.Exp, ['p8c16', 'p8hs'], ['p8e16', 'p8hs'], bias=hs[:, 0:1], accum=hs[:, 1:2])
                        P.act(hs[:, 2:3], hs[:, 1:2], AF.Ln, ['p8hs'], ['p8hs'])
                        P.tt(hs[:, 3:4], hs[:, 0:1], hs[:, 2:3], ALU.subtract, ['p8hs'], ['p8hs'])
                        for ib in range(128 // IB):
                            d_ = cnt_D % 2
                            cnt_D += 1
                            kD, kE, kTb = 'p8D%d' % d_, 'p8E%d' % d_, 'p8T%d' % d_
                            P.tt(Db[d_][:], s1[:, ib * IB:(ib + 1) * IB].unsqueeze(2).broadcast_to([128, IB, 128]),
                                 s2.unsqueeze(1).broadcast_to([128, IB, 128]), ALU.add, [ksc], [kD], eng='gpsimd')
                            P.act(Eb[d_][:], Db[d_][:], AF.Exp, [kD, 'p8hs'], [kE], bias=hs[:, 3:4])
                            P.stt(Tb[d_][:], Db[d_][:], c16[:, 15:16], Eb[d_][:], ALU.is_ge, ALU.mult, [kD, kE, 'p8c16'], [kTb])
                            gsl = G[:, ib * IB * 128:(ib + 1) * IB * 128]
                            P.tt(gsl, gsl, Tb[d_][:].rearrange("p a b -> p (a b)"), ALU.add, [kG, kTb], [kG])
                    P.dma(Gd[ti * 128:(ti + 1) * 128, :], G[:], reads=[kG])
                P.end_phase()
            if upto <= 8:
                break

            EB = 256
            for lb in range(LB):
                with ExitStack() as st:
                    tok0 = lb * L + (CTX if last else 0)
                    ntok = L - (CTX if last else 0)
                    ntile = ntok // 128
                    groups = []
                    o_ = 0
                    while o_ < ntok:
                        groups.append((o_, min(512, ntok - o_)))
                        o_ += 512
                    hT = sbt(st, "p9h", [128, 8, ntok], BF16)
                    acc = sbt(st, "p9acc", [128, ntile, D])
                    for c in range(8):
                        P.dma(hT[:, c, :], H2T[c * 128:(c + 1) * 128, tok0:tok0 + ntok], writes=['p9h'])
                    for tl_ in range(ntile):
                        P.memset(acc[:, tl_, :], 0.0, [('p9acc', tl_)], eng='gpsimd')
                    uts = [sbt(st, "p9u%d" % i, [128, 8, EB], BF16) for i in range(2)]
                    vts = [sbt(st, "p9v%d" % i, [128, 2, D], BF16) for i in range(2)]
                    gts = [sbt(st, "p9g%d" % i, [128, 4, EB], BF16) for i in range(2)]
                    asb = [sbt(st, "p9a%d" % i, [128, 512], BF16) for i in range(2)]
                    wsb_ = [sbt(st, "p9w%d" % i, [128, 2, 512], BF16) for i in range(2)]
                    psa = [pst(st, "p9pa%d" % i, [128, 512]) for i in range(2)]
                    psg = [pst(st, "p9pg%d" % i, [128, 1024], BF16) for i in range(2)]
                    pso = [pst(st, "p9po%d" % i, [128, 512]) for i in range(2)]
                    puTv = puT[l].rearrange("(kc p) e -> p kc e", p=128)
                    ca = 0
                    co = 0
                    cg = 0
                    for eb in range(NEXP // EB):
                        ub, vb_ = uts[eb % 2], vts[eb % 2]
                        ku, kv_ = 'p9u%d' % (eb % 2), 'p9v%d' % (eb % 2)
                        for kc in range(8):
                            P.dma(ub[:, kc, :], puTv[:, kc, eb * EB:(eb + 1) * EB], writes=[ku], queue='gpsimd')
                        P.dma(vb_[:], pv[l][eb * EB:(eb + 1) * EB, :].rearrange("(c p) d -> p c d", p=128), writes=[kv_], queue='gpsimd')
                        for (g0, gn) in groups:
                            gb = cg % 2
                            cg += 1
                            kg, kw = 'p9g%d' % gb, 'p9w%d' % gb
                            nt_g = gn // 128
                            P.dma(gts[gb][:, 0:nt_g, :],
                                  Gd[tok0 + g0:tok0 + g0 + gn, eb * EB:(eb + 1) * EB].rearrange("(t p) e -> p t e", p=128), writes=[kg])
                            for ec in range(2):
                                ab = ca % 2
                                ca += 1
                                kpa, kpg, ka = 'p9pa%d' % ab, 'p9pg%d' % ab, 'p9a%d' % ab
                                for kc in range(8):
                                    P.mm(psa[ab][:, 0:gn], ub[:, kc, ec * 128:(ec + 1) * 128], hT[:, kc, g0:g0 + gn],
                                         kc == 0, kc == 7, [ku, 'p9h'], [kpa])
                                P.act(asb[ab][:, 0:gn], psa[ab][:, 0:gn], AF.Gelu_apprx_tanh, [kpa], [ka])
                                for t_ in range(nt_g):
                                    P.tr(psg[ab][:, t_ * 128:(t_ + 1) * 128], gts[gb][:, t_, ec * 128:(ec + 1) * 128], identb[:],
                                         [kg, 'identb'], [kpg])
                                P.tt(wsb_[gb][:, ec, 0:gn], asb[ab][:, 0:gn], psg[ab][:, 0:gn], ALU.mult, [ka, kpg], [kw])
                            for t_ in range(nt_g):
                                tl = (g0 // 128) + t_
                                for half in range(2):
                                    ob = co % 2
                                    co += 1
                                    kpo = 'p9po%d' % ob
                                    for ec in range(2):
                                        P.mm(pso[ob][:], wsb_[gb][:, ec, t_ * 128:(t_ + 1) * 128], vb_[:, ec, half * 512:(half + 1) * 512],
                                             ec == 0, ec == 1, [kw, kv_], [kpo])
                                    asl = acc[:, tl, half * 512:(half + 1) * 512]
                                    P.tt(asl, pso[ob][:], asl, ALU.add, [kpo, ('p9acc', tl)], [('p9acc', tl)])
                    xts = [sbt(st, "p9x%d" % i, [128, D]) for i in range(2)]
                    if last:
                        fg = sbt(st, "p9fg", [128, D])
                        P.dma(fg[:], fin_g.partition_broadcast(128)[:, 0, :], writes=['p9fg'])
                        xns = [sbt(st, "p9xn%d" % i, [128, D]) for i in range(2)]
                        sms = [sbt(st, "p9s%d" % i, [128, 4]) for i in range(2)]
                    for tl in range(ntile):
                        b = tl % 2
                        ti = (tok0 // 128) + tl
                        _, is_ctx, who = tile_info(ti)
                        kx = 'p9x%d' % b
                        P.dma(xts[b][:], X[ti * 128:(ti + 1) * 128, :], writes=[kx])
                        P.tt(acc[:, tl, :], acc[:, tl, :], g2b[:, who, :], ALU.mult, [('p9acc', tl), 'g2b'], [('p9acc', tl)])
                        P.tt(xts[b][:], xts[b][:], acc[:, tl, :], ALU.add, [kx, ('p9acc', tl)], [kx])
                        if not last:
                            P.dma(X[ti * 128:(ti + 1) * 128, :], xts[b][:], reads=[kx])
                        else:
                            rmsnorm_tile(None, xts[b], kx, xns[b], 'p9xn%d' % b, sms[b], 'p9s%d' % b)
                            P.tt(xns[b][:], xns[b][:], fg[:], ALU.mult, ['p9xn%d' % b, 'p9fg'], ['p9xn%d' % b])
                            orow = lb * SEQ + tl * 128
                            P.dma(out_d[orow:orow + 128, :], xns[b][:], reads=['p9xn%d' % b])
                    P.end_phase()
        P.end_phase()
    print("instructions recorded:", P.nins, {e: P.cnt[e] for e in ENGS})
    return nc


def _na_bias_tables(rpb):
    out = np.full((8, 128, 5, 640), NEG, np.float32)
    r0s = [0, 2, 4, 28, 30]
    qc = np.arange(64)
    c0 = np.clip(qc - 8, 0, 48)
    kc = np.arange(64)
    col_ok = (kc[None, :] >= c0[:, None]) & (kc[None, :] < c0[:, None] + 16)
    dc = np.clip(kc[None, :] - qc[:, None] + 15, 0, 30)
    for ci, r0 in enumerate(r0s):
        s0 = min(max(r0 - 4, 0), 24)
        u0 = min(s0, 22)
        for dq in range(2):
            r = r0 + dq
            st = min(max(r - 4, 0), 24)
            for kr in range(10):
                krow = u0 + kr
                if not (st <= krow < st + 8):
                    continue
                dr = krow - r + 7
                vals = rpb[:, dr, :][:, dc]
                blk = np.where(col_ok[None], vals, np.float32(NEG))
                out[:, dq * 64:(dq + 1) * 64, ci, kr * 64:(kr + 1) * 64] = blk
    return out


def prep_shared(inp):
    f = np.float32
    sh = {}
    sh["ada_w"] = np.ascontiguousarray(inp["ada_w"], f)
    sh["ada_b"] = np.ascontiguousarray(inp["ada_b"].reshape(DEPTH, 1, 6 * D), f)
    sh["ada_bT"] = np.ascontiguousarray(inp["ada_b"].reshape(DEPTH, 48, 128).transpose(0, 2, 1), f)
    sh["n1g"] = np.ascontiguousarray(inp["norm1_g"].reshape(DEPTH, 8, 128).transpose(0, 2, 1), f)
    sh["n2g"] = np.ascontiguousarray(inp["norm2_g"].reshape(DEPTH, 8, 128).transpose(0, 2, 1), f)
    sh["fin_g"] = np.ascontiguousarray(inp["final_g"].reshape(1, D), f)
    sh["w_in"] = np.ascontiguousarray(inp["w_in"], f)
    sh["w_out"] = np.ascontiguousarray(inp["w_out"], f)
    sh["rw_w_up"] = np.ascontiguousarray(inp["rw_w_up"].reshape(DEPTH, 128, 256), f)
    sh["rw_a_up"] = np.ascontiguousarray(inp["rw_a_up"].reshape(DEPTH, 128, 256), f)
    sh["rw_w0"] = np.ascontiguousarray(inp["rw_w0"].reshape(DEPTH, 1, 512), f)
    sh["rw_a0"] = np.ascontiguousarray(inp["rw_a0"].reshape(DEPTH, 1, 512), f)
    sh["rw_g_up"] = np.ascontiguousarray(inp["rw_g_up"], f)
    sh["rw_k_k"] = np.ascontiguousarray(inp["rw_k_k"].reshape(DEPTH, 1, 256), f)
    sh["rw_k_a"] = np.ascontiguousarray(inp["rw_k_a"].reshape(DEPTH, 1, 256), f)
    rk = inp["rw_r_k"].reshape(DEPTH, 4, 64).transpose(0, 2, 1)
    sh["rw_r_k"] = np.ascontiguousarray(np.concatenate([rk, rk], axis=1), f)
    lx = inp["rw_lnx_g"].reshape(DEPTH, 4, 64).transpose(0, 2, 1)
    sh["rw_lnx"] = np.ascontiguousarray(np.concatenate([lx, lx], axis=1), f)
    sh["na_bias"] = np.stack([_na_bias_tables(np.asarray(inp["na_rpb"][l], f)) for l in range(DEPTH)])
    sh["sc_w"] = np.ascontiguousarray(inp["sc_conv_w"].reshape(DEPTH, 3, 2, 128).transpose(0, 3, 2, 1), f)
    sh["pq_w"] = np.ascontiguousarray(inp["peer_q_w"], f)
    sh["pkT"] = np.ascontiguousarray(inp["peer_sub_keys"].reshape(DEPTH, 16, 128, 128).transpose(0, 3, 1, 2), f)
    sh["puT"] = np.ascontiguousarray(inp["peer_u"].transpose(0, 2, 1), f)
    sh["pv"] = np.ascontiguousarray(inp["peer_v"], f)
    sh["ident"] = np.eye(128, dtype=f)
    jb = np.zeros((128, 128), f)
    jb[:64, :64] = 1.0
    jb[64:, 64:] = 1.0
    sh["jblk"] = jb
    return sh


def prep_core(inp, core):
    f = np.float32
    b0 = core * LB
    xs = []
    for lb in range(LB):
        xs.append(inp["ctx"][b0 + lb])
        xs.append(inp["x"][b0 + lb])
    m = {"xin": np.ascontiguousarray(np.concatenate(xs, axis=0), f)}
    cv = np.stack([inp["c"][b0], inp["c"][b0 + 1], inp["c_ctx"]], axis=0)
    m["cT"] = np.ascontiguousarray(cv.reshape(3, 8, 128).transpose(2, 1, 0), f)
    return m


_NC_CACHE = {}


def kernel(**inputs):
    inp = {k: np.asarray(v) for k, v in inputs.items()}
    if 'nc' not in _NC_CACHE:
        _NC_CACHE['nc'] = build()
    nc = _NC_CACHE['nc']
    sh = prep_shared(inp)
    in_maps = []
    for core in range(8):
        m = dict(sh)
        m.update(prep_core(inp, core))
        in_maps.append(m)
    res = run_bass_kernel_spmd(nc, in_maps, core_ids=list(range(8)))
    outs = [np.asarray(r["out"]).reshape(LB, SEQ, D) for r in res.results]
    return np.concatenate(outs, axis=0).astype(np.float32)
```

```python
import numpy as np
import ml_dtypes
from contextlib import ExitStack
import concourse.bass as bass
import concourse.mybir as mybir
from concourse.bass_utils import run_bass_kernel_spmd

F32 = mybir.dt.float32
BF16 = mybir.dt.bfloat16
U32 = mybir.dt.uint32
ALU = mybir.AluOpType
AF = mybir.ActivationFunctionType
AX = mybir.AxisListType

ENGS = ['tensor', 'vector', 'scalar', 'gpsimd', 'sync']
NDMA = 12

D = 1024
DEPTH = 2
LB = 2
CTX = 256
SEQ = 2048
L = CTX + SEQ
T = LB * L
NT = T // 128
TPB = L // 128
D_IN = 3456
NEXP = 16384
NEG = -30000.0
NORM_EPS = 1e-6
GN_EPS = 64e-5


class Prog:
    def __init__(self, nc, stack):
        self.nc = nc
        self.sems = {e: stack.enter_context(nc.semaphore("s_" + e)) for e in ENGS}
        self.cnt = {e: 0 for e in ENGS}
        self.dsem = [stack.enter_context(nc.semaphore("d%d" % i)) for i in range(NDMA)]
        self.dcnt = [0] * NDMA
        self.dnext = 0
        self.known = {e: {} for e in ENGS}
        self.last_w = {}
        self.readers = {}
        self.pending = {e: [] for e in ENGS}
        self.nins = 0

    def _sem(self, sk):
        return self.sems[sk[1]] if sk[0] == 'e' else self.dsem[sk[1]]

    def _deps(self, eng, reads, writes, extra=None):
        deps = {}

        def add(sk, v):
            if v > deps.get(sk, 0):
                deps[sk] = v
        for k in reads:
            lw = self.last_w.get(k)
            if lw:
                add(*lw)
        for k in writes:
            lw = self.last_w.get(k)
            if lw:
                add(*lw)
            for sk, v in self.readers.get(k, {}).items():
                add(sk, v)
        if extra:
            for sk, v in extra:
                add(sk, v)
        out = []
        kn = self.known[eng]
        for sk, v in deps.items():
            if sk == ('e', 'tensor') and eng == 'tensor':
                continue
            if kn.get(sk, 0) >= v:
                continue
            kn[sk] = v
            out.append((sk, v))
        return out

    def _mark(self, mysk, myval, reads, writes):
        for k in reads:
            self.readers.setdefault(k, {})[mysk] = myval
        for k in writes:
            self.last_w[k] = (mysk, myval)
            self.readers[k] = {}

    def op(self, eng, fn, reads=(), writes=()):
        waits = self._deps(eng, reads, writes)
        self.cnt[eng] += 1
        self.pending[eng].append((waits, fn, ('e', eng)))
        self._mark(('e', eng), self.cnt[eng], reads, writes)
        self.nins += 1

    def dma(self, out, in_, reads=(), writes=(), queue='sync', **kw):
        self.dma_fn(lambda e: e.dma_start(out=out, in_=in_, **kw), reads, writes, queue)

    def dma_fn(self, fn, reads=(), writes=(), queue='sync'):
        i = self.dnext % NDMA
        self.dnext += 1
        extra = [(('d', i), self.dcnt[i])] if self.dcnt[i] else None
        waits = self._deps(queue, reads, writes, extra)
        self.dcnt[i] += 16
        self.pending[queue].append((waits, fn, ('d', i)))
        self._mark(('d', i), self.dcnt[i], reads, writes)
        self.nins += 1

    def barrier(self):
        for e in ENGS:
            extra = [(('e', o), self.cnt[o]) for o in ENGS if o != e and self.cnt[o]]
            extra += [(('d', i), self.dcnt[i]) for i in range(NDMA) if self.dcnt[i]]
            waits = self._deps(e, (), (), extra)
            if waits:
                self.pending[e].append((waits, None, None))

    def flush(self):
        nc = self.nc
        with nc.Block() as block:
            for e in ENGS:
                items = self.pending[e]
                if not items:
                    continue

                def body(eng, items=items):
                    for waits, fn, inc in items:
                        for sk, v in waits:
                            eng.wait_ge(self._sem(sk), v)
                        if fn is not None:
                            ins = fn(eng)
                            ins.then_inc(self._sem(inc), 1 if inc[0] == 'e' else 16)
                getattr(block, e)(body)
                self.pending[e] = []

    def end_phase(self):
        self.barrier()
        self.flush()

    def mm(self, out, lhsT, rhs, start, stop, r, w):
        self.op('tensor', lambda e: e.matmul(out, lhsT=lhsT, rhs=rhs, start=start, stop=stop), r, w)

    def tr(self, out, in_, ident, r, w):
        self.op('tensor', lambda e: e.transpose(out, in_, ident), r, w)

    def act(self, out, in_, func, r, w, bias=None, scale=None, accum=None):
        kw = {}
        if bias is not None:
            kw['bias'] = bias
        if scale is not None:
            kw['scale'] = scale
        if accum is not None:
            kw['accum_out'] = accum
        self.op('scalar', lambda e: e.activation(out=out, in_=in_, func=func, **kw), r, w)

    def tt(self, out, in0, in1, op, r, w, eng='vector'):
        self.op(eng, lambda e: e.tensor_tensor(out=out, in0=in0, in1=in1, op=op), r, w)

    def ts(self, out, in0, s1, s2, op0, op1, r, w, eng='vector'):
        if s2 is None:
            self.op(eng, lambda e: e.tensor_scalar(out=out, in0=in0, scalar1=s1, scalar2=None, op0=op0), r, w)
        else:
            self.op(eng, lambda e: e.tensor_scalar(out=out, in0=in0, scalar1=s1, scalar2=s2, op0=op0, op1=op1), r, w)

    def stt(self, out, in0, scalar, in1, op0, op1, r, w, eng='vector'):
        self.op(eng, lambda e: e.scalar_tensor_tensor(out=out, in0=in0, scalar=scalar, in1=in1, op0=op0, op1=op1), r, w)

    def cp(self, out, in_, r, w, eng='vector'):
        if eng == 'scalar':
            self.op(eng, lambda e: e.copy(out=out, in_=in_), r, w)
        else:
            self.op(eng, lambda e: e.tensor_copy(out=out, in_=in_), r, w)

    def red(self, out, in_, op, r, w, eng='vector'):
        self.op(eng, lambda e: e.tensor_reduce(out=out, in_=in_, axis=AX.X, op=op), r, w)

    def memset(self, ap, val, w, eng='vector'):
        self.op(eng, lambda e: e.memset(ap, val), (), w)


def tile_info(ti):
    lb = ti // TPB
    is_ctx = (ti % TPB) < 2
    who = 2 if is_ctx else lb
    return lb, is_ctx, who


def build(dbg=False, upto=99, layers=DEPTH):
    nc = bass.Bass("TRN2", target_bir_lowering=False)

    def din(name, shape, dt=F32):
        return nc.dram_tensor(name, list(shape), dt, kind="ExternalInput").ap()

    def dscr(name, shape, dt=F32):
        return nc.dram_tensor(name, list(shape), dt, kind="ExternalOutput" if dbg else "Internal").ap()

    xin = din("xin", [T, D])
    cT = din("cT", [128, 8, 3])
    ada_w = din("ada_w", [DEPTH, D, 6 * D])
    ada_b = din("ada_b", [DEPTH, 1, 6 * D])
    ada_bT = din("ada_bT", [DEPTH, 128, 48])
    n1g = din("n1g", [DEPTH, 128, 8])
    n2g = din("n2g", [DEPTH, 128, 8])
    fin_g = din("fin_g", [1, D])
    w_in = din("w_in", [DEPTH, D, D_IN])
    w_out = din("w_out", [DEPTH, D, D])
    rw_w_up = din("rw_w_up", [DEPTH, 128, 256])
    rw_a_up = din("rw_a_up", [DEPTH, 128, 256])
    rw_w0 = din("rw_w0", [DEPTH, 1, 512])
    rw_a0 = din("rw_a0", [DEPTH, 1, 512])
    rw_g_up = din("rw_g_up", [DEPTH, 128, 256])
    rw_k_k = din("rw_k_k", [DEPTH, 1, 256])
    rw_k_a = din("rw_k_a", [DEPTH, 1, 256])
    rw_r_k = din("rw_r_k", [DEPTH, 128, 4])
    rw_lnx = din("rw_lnx", [DEPTH, 128, 4])
    na_bias = din("na_bias", [DEPTH, 8, 128, 5, 640])
    sc_w = din("sc_w", [DEPTH, 128, 2, 3])
    pq_w = din("pq_w", [DEPTH, D, 2048])
    pkT = din("pkT", [DEPTH, 128, 16, 128])
    NE_D = NEXP if upto >= 8 else 256
    puT = din("puT", [DEPTH, D, NE_D])
    pv = din("pv", [DEPTH, NE_D, D])
    ident_d = din("ident", [128, 128])
    jblk_d = din("jblk", [128, 128])
    out_d = nc.dram_tensor("out", [LB * SEQ, D], F32, kind="ExternalOutput").ap()

    X = dscr("X", [T, D])
    MODd = dscr("MODd", [3, 6 * D])
    ZTf = dscr("ZTf", [9 * 128, T])
    ZTb = dscr("ZTb", [14 * 128, T], BF16)
    Ztok = dscr("Ztok", [T, 768])
    Vtok = dscr("Vtok", [T, 512], BF16)
    Pd = dscr("Pd", [LB, 2, L, 5, 256])
    YcatT = dscr("YcatT", [D, T], BF16)
    H2T = dscr("H2T", [D, T], BF16)
    Gd = dscr("Gd", [T, NE_D], BF16)

    with ExitStack() as top:
        P = Prog(nc, top)

        uid = [0]

        def sbt(st, name, shape, dt=F32):
            uid[0] += 1
            return st.enter_context(nc.sbuf_tensor("%s_s%d" % (name, uid[0]), list(shape), dt))

        def pst(st, name, shape, dt=F32):
            uid[0] += 1
            return st.enter_context(nc.psum_tensor("%s_p%d" % (name, uid[0]), list(shape), dt))

        ident = sbt(top, "ident", [128, 128])
        identb = sbt(top, "identb", [128, 128], BF16)
        jblk = sbt(top, "jblk", [128, 128])
        scT = sbt(top, "scT", [128, 8, 3])
        modT = sbt(top, "modT", [128, 48, 3])
        A1 = sbt(top, "A1", [128, 8, 3])
        A2 = sbt(top, "A2", [128, 8, 3])
        g1b = sbt(top, "g1b", [128, 3, D])
        g2b = sbt(top, "g2b", [128, 3, D])
        ones1 = sbt(top, "ones1", [1, 128])
        eps_n = sbt(top, "eps_n", [128, 1])

        P.dma(ident[:], ident_d, writes=['ident'])
        P.dma(jblk[:], jblk_d, writes=['jblk'])
        P.dma(scT[:], cT, writes=['scT'])
        P.cp(identb[:], ident[:], ['ident'], ['identb'])
        P.act(scT[:], scT[:], AF.Silu, ['scT'], ['scT'])
        P.memset(ones1[:], 1.0, ['ones1'])
        for i in range(4):
            P.dma(X[i * (T // 4):(i + 1) * (T // 4), :], xin[i * (T // 4):(i + 1) * (T // 4), :], writes=['X'])
        P.end_phase()

        def rmsnorm_tile(st_tiles, xt, key_x, xn, key_xn, small, key_small):
            P.act(xn[:], xt[:], AF.Square, [key_x], [key_xn, key_small], accum=small[:, 0:1])
            P.ts(small[:, 1:2], small[:, 0:1], 1.0 / D, NORM_EPS, ALU.mult, ALU.add, [key_small], [key_small])
            P.act(small[:, 2:3], small[:, 1:2], AF.Sqrt, [key_small], [key_small])
            P.op('vector', lambda e: e.reciprocal(out=small[:, 3:4], in_=small[:, 2:3]), [key_small], [key_small])
            P.ts(xn[:], xt[:], small[:, 3:4], None, ALU.mult, None, [key_x, key_small], [key_xn])

        for l in range(layers):
            last = (l == DEPTH - 1)
            tiles_all = list(range(NT))
            tiles_need = [ti for ti in tiles_all if not (last and tile_info(ti)[1])]

            with ExitStack() as st:
                wbuf = [sbt(st, "adaw%d" % i, [128, 8, 512]) for i in range(2)]
                bT = sbt(st, "adabT", [128, 48])
                brow = sbt(st, "adabrow", [3, 6 * D])
                mrow = sbt(st, "mrow", [3, 6 * D])
                gn1 = sbt(st, "gn1", [128, 8])
                gn2 = sbt(st, "gn2", [128, 8])
                psT = [pst(st, "psT%d" % i, [128, 512]) for i in range(2)]
                psR = [pst(st, "psR%d" % i, [128, 512]) for i in range(2)]
                P.dma(bT[:], ada_bT[l], writes=['bT'])
                P.dma(brow[:], ada_b[l].partition_broadcast(3)[:, 0, :], writes=['brow'])
                P.dma(gn1[:], n1g[l], writes=['gn1'])
                P.dma(gn2[:], n2g[l], writes=['gn2'])
                aw = ada_w[l].rearrange("(kc p) n -> p kc n", p=128)
                for nb in range(12):
                    wb = wbuf[nb % 2]
                    wk = 'adaw%d' % (nb % 2)
                    for kc in range(8):
                        P.dma(wb[:, kc, :], aw[:, kc, nb * 512:(nb + 1) * 512], writes=[wk])
                    pT = psT[nb % 2]
                    pR = psR[nb % 2]
                    kT_ = 'psT%d' % (nb % 2)
                    kR_ = 'psR%d' % (nb % 2)
                    for j in range(4):
                        for kc in range(8):
                            P.mm(pT[:, j * 3:(j + 1) * 3], wb[:, kc, j * 128:(j + 1) * 128], scT[:, kc, :],
                                 kc == 0, kc == 7, [wk, 'scT'], [kT_])
                    for kc in range(8):
                        P.mm(pR[0:3, :], scT[:, kc, :], wb[:, kc, :], kc == 0, kc == 7, [wk, 'scT'], [kR_])
                    for j in range(4):
                        jj = nb * 4 + j
                        P.ts(modT[:, jj, :], pT[:, j * 3:(j + 1) * 3], bT[:, jj:jj + 1], None, ALU.add, None,
                             [kT_, 'bT'], ['modT'])
                    P.tt(mrow[:, nb * 512:(nb + 1) * 512], pR[0:3, :], brow[:, nb * 512:(nb + 1) * 512], ALU.add,
                         [kR_, 'brow'], ['mrow'])
                P.dma(MODd, mrow[:], reads=['mrow'], writes=['MODd'])
                for who in range(3):
                    P.dma(g1b[:, who, :], MODd[who:who + 1, 2 * D:3 * D].partition_broadcast(128)[:, 0, :],
                          reads=['MODd'], writes=['g1b'])
                    P.dma(g2b[:, who, :], MODd[who:who + 1, 5 * D:6 * D].partition_broadcast(128)[:, 0, :],
                          reads=['MODd'], writes=['g2b'])
                for who in range(3):
                    P.stt(A1[:, :, who], modT[:, 8:16, who], 1.0, gn1[:], ALU.add, ALU.mult, ['modT', 'gn1'], ['A1'])
                    P.stt(A2[:, :, who], modT[:, 32:40, who], 1.0, gn2[:], ALU.add, ALU.mult, ['modT', 'gn2'], ['A2'])
                P.end_phase()
            if upto <= 0:
                break

            with ExitStack() as st:
                wsb = sbt(st, "w_in_sb", [128, 8, D_IN], BF16)
                for kc in range(8):
                    P.dma(wsb[:, kc, :], w_in[l][kc * 128:(kc + 1) * 128, :], writes=['wsb'], queue='gpsimd')
                xts = [sbt(st, "p1x%d" % i, [128, D]) for i in range(2)]
                xns = [sbt(st, "p1xn%d" % i, [128, D]) for i in range(2)]
                smalls = [sbt(st, "p1s%d" % i, [128, 4]) for i in range(2)]
                hTs = [sbt(st, "p1hT%d" % i, [128, 8, 512], BF16) for i in range(2)]
                stf = [sbt(st, "p1stf%d" % i, [128, 512]) for i in range(3)]
                stb = [sbt(st, "p1stb%d" % i, [128, 512], BF16) for i in range(3)]
                stt_ = [sbt(st, "p1stt%d" % i, [128, 768]) for i in range(2)]
                stv = [sbt(st, "p1stv%d" % i, [128, 512], BF16) for i in range(2)]
                pT = [pst(st, "p1pT%d" % i, [128, 512]) for i in range(2)]
                pM = [pst(st, "p1pM%d" % i, [128, 512]) for i in range(4)]
                fm_chunks = list(range(0, 17)) + list(range(21, 27))
                cnt_x = 0
                cnt_pT = 0
                cnt_pM = 0
                cnt_f = 0
                cnt_b = 0
                cnt_t = 0
                for g in range(NT // 4):
                    hT = hTs[g % 2]
                    kh = 'p1hT%d' % (g % 2)
                    for q in range(4):
                        ti = g * 4 + q
                        lb, is_ctx, who = tile_info(ti)
                        xi = cnt_x % 2
                        cnt_x += 1
                        xt, xn, sm = xts[xi], xns[xi], smalls[xi]
                        P.dma(xt[:], X[ti * 128:(ti + 1) * 128, :], writes=['p1x%d' % xi])
                        rmsnorm_tile(None, xt, 'p1x%d' % xi, xn, 'p1xn%d' % xi, sm, 'p1s%d' % xi)
                        for half in range(2):
                            pt = pT[cnt_pT % 2]
                            kp = 'p1pT%d' % (cnt_pT % 2)
                            cnt_pT += 1
                            for c4 in range(4):
                                c = half * 4 + c4
                                P.tr(pt[:, c4 * 128:(c4 + 1) * 128], xn[:, c * 128:(c + 1) * 128], ident[:],
                                     ['p1xn%d' % xi, 'ident'], [kp])
                            for c4 in range(4):
                                c = half * 4 + c4
                                P.ts(hT[:, c, q * 128:(q + 1) * 128], pt[:, c4 * 128:(c4 + 1) * 128],
                                     A1[:, c, who:who + 1], modT[:, c, who:who + 1], ALU.mult, ALU.add,
                                     [kp, 'A1', 'modT'], [kh])
                    for n in fm_chunks:
                        pm = pM[cnt_pM % 4]
                        kp = 'p1pM%d' % (cnt_pM % 4)
                        cnt_pM += 1
                        for kc in range(8):
                            P.mm(pm[:], wsb[:, kc, n * 128:(n + 1) * 128], hT[:, kc, :], kc == 0, kc == 7,
                                 ['wsb', kh], [kp])
                        if n < 9:
                            sf = stf[cnt_f % 3]
                            ks = 'p1stf%d' % (cnt_f % 3)
                            cnt_f += 1
                            P.cp(sf[:], pm[:], [kp], [ks], eng='scalar')
                            P.dma(ZTf[n * 128:(n + 1) * 128, g * 512:(g + 1) * 512], sf[:], reads=[ks])
                        else:
                            zi = n - 9 if n < 17 else n - 21 + 8
                            sf = stb[cnt_b % 3]
                            ks = 'p1stb%d' % (cnt_b % 3)
                            cnt_b += 1
                            P.cp(sf[:], pm[:], [kp], [ks], eng='scalar' if (cnt_b % 2) else 'vector')
                            P.dma(ZTb[zi * 128:(zi + 1) * 128, g * 512:(g + 1) * 512], sf[:], reads=[ks])
                    for q in range(4):
                        ti = g * 4 + q
                        si = cnt_t % 2
                        cnt_t += 1
                        for (c0, cw, dst, off) in ((0, 512, 'z', 0), (512, 256, 'z', 512), (2176, 512, 'v', 0)):
                            pm = pM[cnt_pM % 4]
                            kp = 'p1pM%d' % (cnt_pM % 4)
                            cnt_pM += 1
                            for kc in range(8):
                                P.mm(pm[:, 0:cw], hT[:, kc, q * 128:(q + 1) * 128], wsb[:, kc, c0:c0 + cw],
                                     kc == 0, kc == 7, ['wsb', kh], [kp])
                            if dst == 'z':
                                P.cp(stt_[si][:, off:off + cw], pm[:, 0:cw], [kp], ['p1stt%d' % si], eng='vector')
                            else:
                                P.cp(stv[si][:], pm[:, 0:cw], [kp], ['p1stv%d' % si], eng='scalar')
                        P.dma(Ztok[ti * 128:(ti + 1) * 128, :], stt_[si][:], reads=['p1stt%d' % si])
                        P.dma(Vtok[ti * 128:(ti + 1) * 128, :], stv[si][:], reads=['p1stv%d' % si])
                P.end_phase()
            if upto <= 1:
                break

            with ExitStack() as st:
                wup = sbt(st, "wup", [128, 256])
                aup = sbt(st, "aup", [128, 256])
                w0r = sbt(st, "w0r", [1, 512])
                a0r = sbt(st, "a0r", [1, 512])
                kkb = sbt(st, "kkb", [128, 256])
                kab = sbt(st, "kab", [128, 256])
                P.dma(wup[:], rw_w_up[l], writes=['wup'])
                P.dma(aup[:], rw_a_up[l], writes=['aup'])
                P.dma(w0r[:], rw_w0[l], writes=['w0r'])
                P.dma(a0r[:], rw_a0[l], writes=['a0r'])
                P.dma(kkb[:], rw_k_k[l].partition_broadcast(128)[:, 0, :], writes=['kkb'])
                P.dma(kab[:], rw_k_a[l].partition_broadcast(128)[:, 0, :], writes=['kab'])
                NB2 = 2
                zt = [sbt(st, "p2z%d" % i, [128, 768]) for i in range(NB2)]
                lwa = [sbt(st, "p2lw%d" % i, [128, 2, 128]) for i in range(NB2)]
                pr = [sbt(st, "p2pr%d" % i, [128, 5, 2, 256]) for i in range(NB2)]
                asb = [sbt(st, "p2a%d" % i, [128, 2, 256]) for i in range(NB2)]
                tmp = [sbt(st, "p2t%d" % i, [128, 2, 256]) for i in range(NB2)]
                sm = [sbt(st, "p2s%d" % i, [128, 16]) for i in range(NB2)]
                psw = [pst(st, "p2pw%d" % i, [128, 512]) for i in range(2)]
                psa = [pst(st, "p2pa%d" % i, [128, 512]) for i in range(2)]
                for ti in range(NT):
                    b = ti % NB2
                    lb = ti // TPB
                    tl = (ti % TPB) * 128
                    z, lw, prow, a_, t_, s_ = zt[b], lwa[b], pr[b], asb[b], tmp[b], sm[b]
                    kz, klw, kpr, ka, kt, ks = ('p2z%d' % b, 'p2lw%d' % b, 'p2pr%d' % b, 'p2a%d' % b, 'p2t%d' % b, 'p2s%d' % b)
                    pw, pa = psw[b], psa[b]
                    kpw, kpa = 'p2pw%d' % b, 'p2pa%d' % b
                    P.dma(z[:], Ztok[ti * 128:(ti + 1) * 128, :], writes=[kz])
                    P.dma(lw[:, 0, :], ZTf[6 * 128:7 * 128, ti * 128:(ti + 1) * 128], writes=[klw])
                    P.dma(lw[:, 1, :], ZTf[7 * 128:8 * 128, ti * 128:(ti + 1) * 128], writes=[klw])
                    P.act(lw[:, 0, :], lw[:, 0, :], AF.Tanh, [klw], [klw])
                    for d in range(2):
                        P.mm(pw[:, d * 256:(d + 1) * 256], lw[d * 64:(d + 1) * 64, 0, :], wup[d * 64:(d + 1) * 64, :],
                             True, False, [klw, 'wup'], [kpw])
                        P.mm(pw[:, d * 256:(d + 1) * 256], ones1[0:1, :], w0r[0:1, d * 256:(d + 1) * 256],
                             False, True, ['ones1', 'w0r'], [kpw])
                        P.mm(pa[:, d * 256:(d + 1) * 256], lw[d * 64:(d + 1) * 64, 1, :], aup[d * 64:(d + 1) * 64, :],
                             True, False, [klw, 'aup'], [kpa])
                        P.mm(pa[:, d * 256:(d + 1) * 256], ones1[0:1, :], a0r[0:1, d * 256:(d + 1) * 256],
                             False, True, ['ones1', 'a0r'], [kpa])
                    P.act(t_[:].rearrange("p a b -> p (a b)"), pw[:], AF.Sigmoid, [kpw], [kt])
                    P.act(prow[:, 1, :, :], t_[:], AF.Exp, [kt], [kpr], scale=-float(np.exp(-0.5)))
                    P.act(a_[:].rearrange("p a b -> p (a b)"), pa[:], AF.Sigmoid, [kpa], [ka])
                    r_ = z[:, 0:256]
                    k_ = z[:, 256:512]
                    P.tt(prow[:, 0, 0, :], k_, kkb[:], ALU.mult, [kz, 'kkb'], [kpr])
                    P.tt(t_[:, 0, :], prow[:, 0, 0, :], prow[:, 0, 0, :], ALU.mult, [kpr], [kt])
                    P.red(s_[:, 0:4], t_[:, 0, :].rearrange("p (h j) -> p h j", h=4), ALU.add, [kt], [ks])
                    P.ts(s_[:, 4:8], s_[:, 0:4], 1e-12, None, ALU.add, None, [ks], [ks])
                    P.act(s_[:, 8:12], s_[:, 4:8], AF.Sqrt, [ks], [ks])
                    P.op('vector', lambda e, s_=s_: e.reciprocal(out=s_[:, 12:16], in_=s_[:, 8:12]), [ks], [ks])
                    P.tt(prow[:, 0, 0, :].rearrange("p (h j) -> p h j", h=4),
                         prow[:, 0, 0, :].rearrange("p (h j) -> p h j", h=4),
                         s_[:, 12:16].unsqueeze(2).broadcast_to([128, 4, 64]), ALU.mult, [kpr, ks], [kpr])
                    P.cp(prow[:, 0, 1, :], prow[:, 0, 0, :], [kpr], [kpr], eng='gpsimd')
                    P.cp(prow[:, 4, 0, :], r_, [kz], [kpr], eng='gpsimd')
                    P.cp(prow[:, 4, 1, :], r_, [kz], [kpr], eng='gpsimd')
                    P.tt(prow[:, 2, :, :], a_[:], prow[:, 0, :, :], ALU.mult, [ka, kpr], [kpr])
                    P.stt(t_[:], a_[:], -1.0, kab[:].unsqueeze(1).broadcast_to([128, 2, 256]), ALU.add, ALU.mult,
                          [ka, 'kab'], [kt])
                    P.stt(prow[:, 3, :, :], t_[:], 1.0, k_.unsqueeze(1).broadcast_to([128, 2, 256]), ALU.add, ALU.mult,
                          [kt, kz], [kpr])
                    for d in range(2):
                        P.dma(Pd[lb, d, tl:tl + 128], prow[:, :, d, :], reads=[kpr])
                P.end_phase()
            if upto <= 2:
                break

            with ExitStack() as st:
                Y = sbt(st, "scanY", [128, 2, 4, L])
                with ExitStack() as st3:
                    Vs = sbt(st3, "scanV", [128, 4, L])
                    S = sbt(st3, "scanS", [128, 2, 256])
                    NST = 2
                    bcs = [sbt(st3, "scanB%d" % i, [128, NST, 2, 5, 256]) for i in range(2)]
                    t1 = sbt(st3, "scanT1", [128, 2, 256])
                    t2 = [sbt(st3, "scanT2%d" % i, [128, 2, 256]) for i in range(2)]
                    sa = sbt(st3, "scanSa", [128, 8])
                    for lb in range(LB):
                        src = bass.AP(ZTf.tensor, 512 * T + lb * L, [[T, 64], [64 * T, 4], [1, L]])
                        P.dma(Vs[lb * 64:(lb + 1) * 64, :, :], src, writes=['scanV'])
                    P.memset(S[:], 0.0, ['scanS'])
                    S3 = S[:].rearrange("p d (h j) -> p (d h) j", h=4)
                    ngroups = L // NST
                    for gi in range(ngroups):
                        bc = bcs[gi % 2]
                        kb = 'scanB%d' % (gi % 2)
                        s0 = gi * NST
                        for lb in range(LB):
                            for d in range(2):
                                if d == 0:
                                    tok0, stp = s0, 1
                                else:
                                    tok0 = (CTX - 1 - s0) if s0 < CTX else (L + CTX - 1 - s0)
                                    stp = -1
                                src = bass.AP(Pd.tensor, ((lb * 2 + d) * L + tok0) * 1280,
                                              [[0, 64], [stp * 1280, NST], [1, 1280]])
                                P.dma(bc[lb * 64:(lb + 1) * 64, :, d, :, :].rearrange("p k v c -> p k (v c)"), src, writes=[kb])
                        for k in range(NST):
                            s = s0 + k
                            tk0 = s
                            tk1 = (CTX - 1 - s) if s < CTX else (L + CTX - 1 - s)
                            kkv, wv, bv, kv, rv = (bc[:, k, :, i, :] for i in range(5))
                            tb = t2[s % 2]
                            ktb = 'scanT2%d' % (s % 2)
                            vap = bass.AP(Vs, Vs[:].offset + tk0, [[4 * L, 128], [tk1 - tk0, 2], [L, 4], [0, 64]])
                            P.tt(tb[:].rearrange("p d (h j) -> p d h j", h=4), vap,
                                 kv.rearrange("p d (h j) -> p d h j", h=4), ALU.mult, ['scanV', kb], [ktb], eng='gpsimd')
                            P.tt(t1[:], S[:], kkv, ALU.mult, ['scanS', kb], ['scanT1'])
                            P.red(sa[:], t1[:].rearrange("p d (h j) -> p (d h) j", h=4), ALU.add, ['scanT1'], ['scanSa'])
                            P.tt(S[:], S[:], wv, ALU.mult, ['scanS', kb], ['scanS'])
                            P.tt(t1[:].rearrange("p d (h j) -> p d h j", h=4),
                                 bv.rearrange("p d (h j) -> p d h j", h=4),
                                 sa[:].rearrange("p (d h) -> p d h", d=2).unsqueeze(3).broadcast_to([128, 2, 4, 64]), ALU.mult, ['scanSa', kb], ['scanT1'])
                            P.tt(S[:], S[:], t1[:], ALU.subtract, ['scanS', 'scanT1'], ['scanS'])
                            P.tt(S[:], S[:], tb[:], ALU.add, ['scanS', ktb], ['scanS'])
                            P.tt(t1[:], S[:], rv, ALU.mult, ['scanS', kb], ['scanT1'])
                            P.red(Y[:, :, :, s], t1[:].rearrange("p d (h j) -> p d h j", h=4), ALU.add, ['scanT1'], ['scanY'])
                    P.end_phase()
                if upto <= 3:
                    break
                P.tt(Y[:, 0, :, 0:CTX], Y[:, 0, :, 0:CTX], Y[:, 1, :, CTX - 1::-1] if False else
                     bass.AP(Y, Y[:].offset + 4 * L + CTX - 1, [[8 * L, 128], [L, 4], [-1, CTX]]),
                     ALU.add, ['scanY'], ['scanY'])
                P.tt(Y[:, 0, :, CTX:L], Y[:, 0, :, CTX:L],
                     bass.AP(Y, Y[:].offset + 4 * L + L - 1, [[8 * L, 128], [L, 4], [-1, SEQ]]),
                     ALU.add, ['scanY'], ['scanY'])
                with ExitStack() as st4:
                    BW = 384
                    gupA = sbt(st4, "gupA", [128, 4, 128])
                    gupB = sbt(st4, "gupB", [128, 4, 128])
                    rk_ = sbt(st4, "rwrk", [128, 4])
                    lnx = sbt(st4, "rwlnx", [128, 4])
                    P.memset(gupA[:], 0.0, ['gupA'])
                    P.memset(gupB[:], 0.0, ['gupB'])
                    gsrc = rw_g_up[l].rearrange("r (h i) -> r h i", h=4)
                    P.dma(gupA[:, :, 0:64], gsrc, writes=['gupA'])
                    P.dma(gupB[:, :, 64:128], gsrc, writes=['gupB'])
                    P.dma(rk_[:], rw_r_k[l], writes=['rwrk'])
                    P.dma(lnx[:], rw_lnx[l], writes=['rwlnx'])
                    NB4 = 2
                    rb = [sbt(st4, "p4r%d" % i, [128, BW]) for i in range(NB4)]
                    kb_ = [sbt(st4, "p4k%d" % i, [128, BW]) for i in range(NB4)]
                    vb = [sbt(st4, "p4v%d" % i, [128, BW]) for i in range(NB4)]
                    sg = [sbt(st4, "p4sg%d" % i, [128, 2, BW]) for i in range(NB4)]
                    yc = [sbt(st4, "p4yc%d" % i, [128, BW]) for i in range(NB4)]
                    sq = [sbt(st4, "p4sq%d" % i, [128, BW]) for i in range(NB4)]
                    yo = [sbt(st4, "p4yo%d" % i, [128, BW], BF16) for i in range(NB4)]
                    pm_ = [pst(st4, "p4pm%d" % i, [128, 512]) for i in range(2)]
                    pv_ = [pst(st4, "p4pv%d" % i, [128, 512]) for i in range(2)]
                    pb_ = [pst(st4, "p4pb%d" % i, [128, 512]) for i in range(2)]
                    pg_ = [pst(st4, "p4pg%d" % i, [128, 512]) for i in range(2)]
                    it = 0
                    for h in range(4):
                        for blk in range(L // BW):
                            b = it % NB4
                            it += 1
                            c0 = blk * BW
                            kr, kk_, kv_, ksg, kyc, ksq, kyo = ('p4r%d' % b, 'p4k%d' % b, 'p4v%d' % b, 'p4sg%d' % b,
                                                               'p4yc%d' % b, 'p4sq%d' % b, 'p4yo%d' % b)
                            kpm, kpv, kpb, kpg = 'p4pm%d' % b, 'p4pv%d' % b, 'p4pb%d' % b, 'p4pg%d' % b
                            for lb in range(LB):
                                for (dst, kd, row0) in ((rb[b], kr, 0), (kb_[b], kk_, 256), (vb[b], kv_, 512)):
                                    P.dma(dst[lb * 64:(lb + 1) * 64, :],
                                          ZTf[row0 + h * 64:row0 + (h + 1) * 64, lb * L + c0:lb * L + c0 + BW], writes=[kd])
                                P.dma(sg[b][:, lb, :], ZTf[8 * 128:9 * 128, lb * L + c0:lb * L + c0 + BW], writes=[ksg])
                            ys = Y[:, 0, h, c0:c0 + BW]
                            P.mm(pm_[b][:, 0:BW], jblk[:], ys, True, True, ['jblk', 'scanY'], [kpm])
                            P.stt(yc[b][:], pm_[b][:, 0:BW], -1.0 / 64, ys, ALU.mult, ALU.add, [kpm, 'scanY'], [kyc])
                            P.act(sq[b][:], yc[b][:], AF.Square, [kyc], [ksq])
                            P.mm(pv_[b][:, 0:BW], jblk[:], sq[b][:], True, True, ['jblk', ksq], [kpv])
                            P.ts(sq[b][:], pv_[b][:, 0:BW], 1.0 / 64, GN_EPS, ALU.mult, ALU.add, [kpv], [ksq])
                            P.act(sq[b][:], sq[b][:], AF.Sqrt, [ksq], [ksq])
                            P.op('vector', lambda e, o=sq[b]: e.reciprocal(out=o[:], in_=o[:]), [ksq], [ksq])
                            P.stt(yc[b][:], yc[b][:], lnx[:, h:h + 1], sq[b][:], ALU.mult, ALU.mult, [kyc, ksq, 'rwlnx'], [kyc])
                            P.stt(rb[b][:], rb[b][:], rk_[:, h:h + 1], kb_[b][:], ALU.mult, ALU.mult, [kr, kk_, 'rwrk'], [kr])
                            P.mm(pb_[b][:, 0:BW], jblk[:], rb[b][:], True, True, ['jblk', kr], [kpb])
                            P.tt(vb[b][:], pb_[b][:, 0:BW], vb[b][:], ALU.mult, [kpb, kv_], [kv_])
                            P.tt(yc[b][:], yc[b][:], vb[b][:], ALU.add, [kyc, kv_], [kyc], eng='gpsimd')
                            P.act(sg[b][:], sg[b][:], AF.Sigmoid, [ksg], [ksg])
                            P.mm(pg_[b][:, 0:BW], gupA[:, h, :], sg[b][:, 0, :], True, False, ['gupA', ksg], [kpg])
                            P.mm(pg_[b][:, 0:BW], gupB[:, h, :], sg[b][:, 1, :], False, True, ['gupB', ksg], [kpg])
                            P.tt(yo[b][:], pg_[b][:, 0:BW], yc[b][:], ALU.mult, [kpg, kyc], [kyo])
                            for lb in range(LB):
                                P.dma(YcatT[h * 64:(h + 1) * 64, lb * L + c0:lb * L + c0 + BW],
                                      yo[b][lb * 64:(lb + 1) * 64, :], reads=[kyo])
                    P.end_phase()
            if upto <= 4:
                break

            with ExitStack() as st:
                qT = sbt(st, "naq", [128, 4, L], BF16)
                kT = sbt(st, "nak", [128, 4, L], BF16)
                Vt = sbt(st, "nav", [128, TPB, 512], BF16)
                ya = sbt(st, "naya", [128, TPB, 512], BF16)
                bias = [sbt(st, "nab%d" % i, [128, 5, 640]) for i in range(2)]
                NB5 = 2
                s_sb = [sbt(st, "nas%d" % i, [128, 896]) for i in range(NB5)]
                p_sb = [sbt(st, "nap%d" % i, [128, 896], BF16) for i in range(NB5)]
                pT_sb = [sbt(st, "napT%d" % i, [128, 7, 128], BF16) for i in range(NB5)]
                sm5 = [sbt(st, "nasm%d" % i, [128, 4]) for i in range(NB5)]
                yT_sb = [sbt(st, "nayT%d" % i, [128, 4, 128], BF16) for i in range(2)]
                psA = [pst(st, "napsA%d" % i, [128, 512]) for i in range(2)]
                psB = [pst(st, "napsB%d" % i, [128, 512]) for i in range(2)]
                psT5 = [pst(st, "napsT%d" % i, [128, 1024], BF16) for i in range(2)]
                psO = [pst(st, "napsO%d" % i, [128, 512]) for i in range(2)]
                it = 0
                ito = 0
                for lb in range(LB):
                    for c in range(4):
                        P.dma(qT[:, c, :], ZTb[c * 128:(c + 1) * 128, lb * L:(lb + 1) * L], writes=['naq'])
                        P.dma(kT[:, c, :], ZTb[(4 + c) * 128:(5 + c) * 128, lb * L:(lb + 1) * L], writes=['nak'])
                    P.dma(Vt[:], Vtok[lb * L:(lb + 1) * L, :].rearrange("(t p) c -> p t c", p=128), writes=['nav'])
                    for h in range(8):
                        bs = bias[h % 2]
                        kbs = 'nab%d' % (h % 2)
                        P.dma(bs[:], na_bias[l, h], writes=[kbs])
                        hp = (h % 2) * 64
                        hc = h // 2
                        for qt in range(TPB):
                            if qt < 2 and last:
                                continue
                            b = it % NB5
                            it += 1
                            ks, kp, kpT, ksm = 'nas%d' % b, 'nap%d' % b, 'napT%d' % b, 'nasm%d' % b
                            kA, kB, kT5, kO = 'napsA%d' % b, 'napsB%d' % b, 'napsT%d' % b, 'napsO%d' % b
                            qh = qT[hp:hp + 64, hc, qt * 128:(qt + 1) * 128]
                            if qt >= 2:
                                r0 = 2 * (qt - 2)
                                s0_ = min(max(r0 - 4, 0), 24)
                                u0 = min(s0_, 22)
                                cfg = {0: 0, 2: 1, 28: 3, 30: 4}.get(r0, 2)
                                wt0 = CTX + u0 * 64
                                nk = 896
                                P.mm(psA[b][:, 0:512], qh, kT[hp:hp + 64, hc, wt0:wt0 + 512], True, True, ['naq', 'nak'], [kA])
                                P.mm(psB[b][:, 0:128], qh, kT[hp:hp + 64, hc, wt0 + 512:wt0 + 640], True, True, ['naq', 'nak'], [kB])
                                P.mm(psB[b][:, 128:384], qh, kT[hp:hp + 64, hc, 0:CTX], True, True, ['naq', 'nak'], [kB])
                                P.stt(s_sb[b][:, 0:512], psA[b][:, 0:512], 0.125, bs[:, cfg, 0:512], ALU.mult, ALU.add,
                                      [kA, kbs], [ks])
                                P.stt(s_sb[b][:, 512:640], psB[b][:, 0:128], 0.125, bs[:, cfg, 512:640], ALU.mult, ALU.add,
                                      [kB, kbs], [ks])
                                P.ts(s_sb[b][:, 640:896], psB[b][:, 128:384], 0.125, None, ALU.mult, None, [kB], [ks])
                                vtiles = [2 + u0 // 2 + j for j in range(5)] + [0, 1]
                            else:
                                nk = 256
                                P.mm(psB[b][:, 128:384], qh, kT[hp:hp + 64, hc, 0:CTX], True, True, ['naq', 'nak'], [kB])
                                P.ts(s_sb[b][:, 0:256], psB[b][:, 128:384], 0.125, None, ALU.mult, None, [kB], [ks])
                                vtiles = [0, 1]
                            sm = sm5[b]
                            P.op('vector', lambda e, o=sm, i_=s_sb[b], nk=nk: e.tensor_reduce(out=o[:, 0:1], in_=i_[:, 0:nk], axis=AX.X, op=ALU.max),
                                 [ks], [ksm])
                            P.ts(sm[:, 1:2], sm[:, 0:1], -1.0, None, ALU.mult, None, [ksm], [ksm])
                            P.act(p_sb[b][:, 0:nk], s_sb[b][:, 0:nk], AF.Exp, [ks, ksm], [kp, ksm], bias=sm[:, 1:2], accum=sm[:, 2:3])
                            P.op('vector', lambda e, o=sm: e.reciprocal(out=o[:, 3:4], in_=o[:, 2:3]), [ksm], [ksm])
                            nblk = nk // 128
                            for j in range(nblk):
                                P.tr(psT5[b][:, j * 128:(j + 1) * 128], p_sb[b][:, j * 128:(j + 1) * 128], identb[:],
                                     [kp, 'identb'], [kT5])
                            P.cp(pT_sb[b][:, 0:nblk, :].rearrange("p a b -> p (a b)"), psT5[b][:, 0:nk], [kT5], [kpT],
                                 eng='scalar')
                            for j in range(nblk):
                                P.mm(psO[b][:, 0:64], pT_sb[b][:, j, :], Vt[:, vtiles[j], h * 64:(h + 1) * 64],
                                     j == 0, j == nblk - 1, [kpT, 'nav'], [kO])
                            P.ts(ya[:, qt, h * 64:(h + 1) * 64], psO[b][:, 0:64], sm[:, 3:4], None, ALU.mult, None,
                                 [kO, ksm], ['naya'])
                    for qt in range(TPB):
                        if qt < 2 and last:
                            continue
                        b = ito % 2
                        ito += 1
                        kT5, kyT = 'napsT%d' % b, 'nayT%d' % b
                        for c in range(4):
                            P.tr(psT5[b][:, c * 128:(c + 1) * 128], ya[:, qt, c * 128:(c + 1) * 128], identb[:],
                                 ['naya', 'identb'], [kT5])
                        P.cp(yT_sb[b][:].rearrange("p a b -> p (a b)"), psT5[b][:, 0:512], [kT5], [kyT])
                        col = lb * L + qt * 128
                        P.dma(YcatT[256:768, col:col + 128].rearrange("(c p) t -> p c t", p=128), yT_sb[b][:], reads=[kyT])
                P.end_phase()
            if upto <= 5:
                break

            with ExitStack() as st:
                scw = sbt(st, "scw", [128, 2, 3])
                P.dma(scw[:], sc_w[l], writes=['scw'])
                HT_ = T // 2
                for cc in range(2):
                    for hf in range(LB):
                        bb = sbt(st, "scb%d%d" % (cc, hf), [128, L], BF16)
                        cb = sbt(st, "scc%d%d" % (cc, hf), [128, L], BF16)
                        xb = sbt(st, "scx%d%d" % (cc, hf), [128, L], BF16)
                        u = sbt(st, "scu%d%d" % (cc, hf), [128, L])
                        o = sbt(st, "sco%d%d" % (cc, hf), [128, L])
                        ob = sbt(st, "scob%d%d" % (cc, hf), [128, L], BF16)
                        kk = "%d%d" % (cc, hf)
                        cs = slice(hf * L, (hf + 1) * L)
                        P.dma(bb[:], ZTb[(8 + cc) * 128:(9 + cc) * 128, cs], writes=['scb' + kk])
                        P.dma(cb[:], ZTb[(10 + cc) * 128:(11 + cc) * 128, cs], writes=['scc' + kk])
                        P.dma(xb[:], ZTb[(12 + cc) * 128:(13 + cc) * 128, cs], writes=['scx' + kk])
                        P.tt(u[:], cb[:], xb[:], ALU.mult, ['scc' + kk, 'scx' + kk], ['scu' + kk])
                        P.ts(o[:], u[:], scw[:, cc, 1:2], None, ALU.mult, None, ['scu' + kk, 'scw'], ['sco' + kk])
                        for (a, e_) in ((0, CTX), (CTX, L)):
                            P.stt(o[:, a + 1:e_], u[:, a:e_ - 1], scw[:, cc, 0:1], o[:, a + 1:e_], ALU.mult, ALU.add,
                                  ['scu' + kk, 'scw', 'sco' + kk], ['sco' + kk])
                            P.stt(o[:, a:e_ - 1], u[:, a + 1:e_], scw[:, cc, 2:3], o[:, a:e_ - 1], ALU.mult, ALU.add,
                                  ['scu' + kk, 'scw', 'sco' + kk], ['sco' + kk])
                        P.tt(ob[:], o[:], bb[:], ALU.mult, ['sco' + kk, 'scb' + kk], ['scob' + kk])
                        P.dma(YcatT[(6 + cc) * 128:(7 + cc) * 128, cs], ob[:], reads=['scob' + kk])
                P.end_phase()
            if upto <= 6:
                break

            with ExitStack() as st:
                wo = sbt(st, "wo_sb", [128, 8, D], BF16)
                for kc in range(8):
                    P.dma(wo[:, kc, :], w_out[l][kc * 128:(kc + 1) * 128, :], writes=['wo'], queue='gpsimd')
                ycs = [sbt(st, "p7yc%d" % i, [128, 8, 128], BF16) for i in range(2)]
                xts = [sbt(st, "p7x%d" % i, [128, D]) for i in range(2)]
                xo = [sbt(st, "p7xo%d" % i, [128, D]) for i in range(2)]
                pso = [pst(st, "p7ps%d" % i, [128, 512]) for i in range(4)]
                for it, ti in enumerate(tiles_need):
                    b = it % 2
                    lb, is_ctx, who = tile_info(ti)
                    P.dma(ycs[b][:], YcatT[:, ti * 128:(ti + 1) * 128].rearrange("(c p) t -> p c t", p=128), writes=['p7yc%d' % b])
                    P.dma(xts[b][:], X[ti * 128:(ti + 1) * 128, :], writes=['p7x%d' % b])
                    for half in range(2):
                        pp = pso[(it * 2 + half) % 4]
                        kp = 'p7ps%d' % ((it * 2 + half) % 4)
                        for kc in range(8):
                            P.mm(pp[:], ycs[b][:, kc, :], wo[:, kc, half * 512:(half + 1) * 512], kc == 0, kc == 7,
                                 ['p7yc%d' % b, 'wo'], [kp])
                        P.tt(xo[b][:, half * 512:(half + 1) * 512], pp[:], g1b[:, who, half * 512:(half + 1) * 512], ALU.mult,
                             [kp, 'g1b'], ['p7xo%d' % b])
                    P.tt(xo[b][:], xo[b][:], xts[b][:], ALU.add, ['p7xo%d' % b, 'p7x%d' % b], ['p7xo%d' % b], eng='gpsimd')
                    P.dma(X[ti * 128:(ti + 1) * 128, :], xo[b][:], reads=['p7xo%d' % b])
                P.end_phase()
            if upto <= 7:
                break

            with ExitStack() as st:
                qw = sbt(st, "qw_sb", [128, 8, 2048], BF16)
                for kc in range(8):
                    P.dma(qw[:, kc, :], pq_w[l][kc * 128:(kc + 1) * 128, :], writes=['qw'], queue='gpsimd')
                keys = sbt(st, "pk_sb", [128, 16, 128], BF16)
                P.dma(keys[:], pkT[l], writes=['pk'], queue='gpsimd')
                xts = [sbt(st, "p8x%d" % i, [128, D]) for i in range(2)]
                xns = [sbt(st, "p8xn%d" % i, [128, D]) for i in range(2)]
                sms = [sbt(st, "p8s%d" % i, [128, 4]) for i in range(2)]
                h2 = [sbt(st, "p8h%d" % i, [128, 8, 128], BF16) for i in range(2)]
                qTs = [sbt(st, "p8q%d" % i, [128, 16, 128], BF16) for i in range(2)]
                ssb = [sbt(st, "p8sc%d" % i, [128, 2048]) for i in range(2)]
                Gs = [sbt(st, "p8G%d" % i, [128, NEXP], BF16) for i in range(1)]
                wk = sbt(st, "p8wk", [128, 256])
                t16 = sbt(st, "p8t16", [128, 2, 16])
                cand = sbt(st, "p8cand", [128, 256])
                c16s = [sbt(st, "p8c16%d" % i, [128, 8, 16]) for i in range(2)]
                e16 = sbt(st, "p8e16", [128, 16])
                hss = [sbt(st, "p8hs%d" % i, [128, 8, 4]) for i in range(2)]
                IB = 16
                Db = [sbt(st, "p8D%d" % i, [128, IB, 128]) for i in range(3)]
                Eb = [sbt(st, "p8E%d" % i, [128, IB, 128]) for i in range(3)]
                Tb = [sbt(st, "p8T%d" % i, [128, IB * 128], BF16) for i in range(3)]
                pT = [pst(st, "p8pT%d" % i, [128, 512]) for i in range(1)]
                pQ = [pst(st, "p8pQ%d" % i, [128, 512]) for i in range(1)]
                pS = [pst(st, "p8pS%d" % i, [128, 512]) for i in range(2)]
                pG = [pst(st, "p8pG%d" % i, [128, 512]) for i in range(4)]
                cnt_D = 0
                for it, ti in enumerate(tiles_need):
                    b = it % 2
                    lb, is_ctx, who = tile_info(ti)
                    xt, xn, sm = xts[b], xns[b], sms[b]
                    kx, kxn, ksm, kh, kq, ksc = ('p8x%d' % b, 'p8xn%d' % b, 'p8s%d' % b, 'p8h%d' % b, 'p8q%d' % b,
                                                 'p8sc%d' % b)
                    c16, hs = c16s[b], hss[b]
                    kc16, khs = 'p8c16%d' % b, 'p8hs%d' % b
                    P.dma(xt[:], X[ti * 128:(ti + 1) * 128, :], writes=[kx])
                    rmsnorm_tile(None, xt, kx, xn, kxn, sm, ksm)
                    for half in range(2):
                        pt = pT[0]
                        kp = 'p8pT0'
                        for c4 in range(4):
                            c = half * 4 + c4
                            P.tr(pt[:, c4 * 128:(c4 + 1) * 128], xn[:, c * 128:(c + 1) * 128], ident[:], [kxn, 'ident'], [kp])
                        for c4 in range(4):
                            c = half * 4 + c4
                            P.ts(h2[b][:, c, :], pt[:, c4 * 128:(c4 + 1) * 128], A2[:, c, who:who + 1],
                                 modT[:, 24 + c, who:who + 1], ALU.mult, ALU.add, [kp, 'A2', 'modT'], [kh])
                    P.dma(H2T[:, ti * 128:(ti + 1) * 128].rearrange("(c p) t -> p c t", p=128), h2[b][:], reads=[kh])
                    for c4g in range(4):
                        pq = pQ[0]
                        kp = 'p8pQ0'
                        for c4 in range(4):
                            c = c4g * 4 + c4
                            for kc in range(8):
                                P.mm(pq[:, c4 * 128:(c4 + 1) * 128], qw[:, kc, c * 128:(c + 1) * 128], h2[b][:, kc, :],
                                     kc == 0, kc == 7, ['qw', kh], [kp])
                        P.cp(qTs[b][:, c4g * 4:(c4g + 1) * 4, :].rearrange("p a b -> p (a b)"), pq[:], [kp], [kq], eng='scalar')
                    for c4g in range(4):
                        pp = pS[c4g % 2]
                        kp = 'p8pS%d' % (c4g % 2)
                        for c4 in range(4):
                            c = c4g * 4 + c4
                            P.mm(pp[:, c4 * 128:(c4 + 1) * 128], qTs[b][:, c, :], keys[:, c, :], True, True, [kq, 'pk'], [kp])
                        P.cp(ssb[b][:, c4g * 512:(c4g + 1) * 512], pp[:], [kp], [ksc], eng='scalar')
                    G = Gs[0]
                    kG = 'p8G0'
                    for h in range(8):
                        s1 = ssb[b][:, h * 256:h * 256 + 128]
                        s2 = ssb[b][:, h * 256 + 128:h * 256 + 256]
                        for p_, sx in ((0, s1), (1, s2)):
                            P.op('vector', lambda e, sx=sx, p_=p_: e.max(out=t16[:, p_, 0:8], in_=sx), [ksc], ['p8t16'])
                            P.op('vector', lambda e, sx=sx, p_=p_: e.match_replace(out=wk[:, 0:128], in_to_replace=t16[:, p_, 0:8],
                                                                              in_values=sx, imm_value=-1e30), [ksc, 'p8t16'], ['p8wk'])
                            P.op('vector', lambda e, p_=p_: e.max(out=t16[:, p_, 8:16], in_=wk[:, 0:128]), ['p8wk'], ['p8t16'])
                        P.tt(cand[:].rearrange("p (a b) -> p a b", a=16), t16[:, 0, :].unsqueeze(2).broadcast_to([128, 16, 16]),
                             t16[:, 1, :].unsqueeze(1).broadcast_to([128, 16, 16]), ALU.add, ['p8t16'], ['p8cand'])
                        P.op('vector', lambda e, c16=c16, h=h: e.max(out=c16[:, h, 0:8], in_=cand[:]), ['p8cand'], [kc16])
                        P.op('vector', lambda e, c16=c16, h=h: e.match_replace(out=wk[:], in_to_replace=c16[:, h, 0:8], in_values=cand[:], imm_value=-1e30),
                             ['p8cand', kc16], ['p8wk'])
                        P.op('vector', lambda e, c16=c16, h=h: e.max(out=c16[:, h, 8:16], in_=wk[:]), ['p8wk'], [kc16])
                        P.ts(hs[:, h, 0:1], c16[:, h, 0:1], -1.0, None, ALU.mult, None, [kc16], [khs])
                        P.act(e16[:], c16[:, h, :], AF.Exp, [kc16, khs], ['p8e16', khs], bias=hs[:, h, 0:1], accum=hs[:, h, 1:2])
                        P.act(hs[:, h, 2:3], hs[:, h, 1:2], AF.Ln, [khs], [khs])
                        P.tt(hs[:, h, 3:4], hs[:, h, 0:1], hs[:, h, 2:3], ALU.subtract, [khs], [khs])
                    for ib in range(128 // IB):
                        for h in range(8):
                            s1 = ssb[b][:, h * 256:h * 256 + 128]
                            s2 = ssb[b][:, h * 256 + 128:h * 256 + 256]
                            d_ = cnt_D % 3
                            cnt_D += 1
                            kD, kE, kTb = 'p8D%d' % d_, 'p8E%d' % d_, 'p8T%d' % d_
                            P.tt(Db[d_][:], s1[:, ib * IB:(ib + 1) * IB].unsqueeze(2).broadcast_to([128, IB, 128]),
                                 s2.unsqueeze(1).broadcast_to([128, IB, 128]), ALU.add, [ksc], [kD], eng='gpsimd')
                            P.act(Eb[d_][:], Db[d_][:], AF.Exp, [kD, khs], [kE], bias=hs[:, h, 3:4])
                            P.stt(Tb[d_][:].rearrange("p (a b) -> p a b", a=IB), Db[d_][:], c16[:, h, 15:16], Eb[d_][:],
                                  ALU.is_ge, ALU.mult, [kD, kE, kc16], [kTb])
                            for q in range(4):
                                P.mm(pG[q][:], identb[:], Tb[d_][:, q * 512:(q + 1) * 512], h == 0, h == 7,
                                     ['identb', kTb], ['p8pG%d' % q])
                        for q in range(4):
                            c0 = ib * IB * 128 + q * 512
                            P.cp(G[:, c0:c0 + 512], pG[q][:], ['p8pG%d' % q], [kG], eng='scalar')
                    P.dma(Gd[ti * 128:(ti + 1) * 128, :], G[:], reads=[kG])
                P.end_phase()
            if upto <= 8:
                break

            EB = 256
            for lb in range(LB):
                with ExitStack() as st:
                    tok0 = lb * L + (CTX if last else 0)
                    ntok = L - (CTX if last else 0)
                    ntile = ntok // 128
                    groups = []
                    o_ = 0
                    while o_ < ntok:
                        groups.append((o_, min(512, ntok - o_)))
                        o_ += 512
                    hT = sbt(st, "p9h", [128, 8, ntok], BF16)
                    acc = sbt(st, "p9acc", [128, ntile, D])
                    for c in range(8):
                        P.dma(hT[:, c, :], H2T[c * 128:(c + 1) * 128, tok0:tok0 + ntok], writes=['p9h'])
                    for tl_ in range(ntile):
                        P.memset(acc[:, tl_, :], 0.0, [('p9acc', tl_)], eng='gpsimd')
                    uts = [sbt(st, "p9u%d" % i, [128, 8, EB], BF16) for i in range(2)]
                    vts = [sbt(st, "p9v%d" % i, [128, 2, D], BF16) for i in range(2)]
                    gts = [sbt(st, "p9g%d" % i, [128, 4, EB], BF16) for i in range(2)]
                    asb = [sbt(st, "p9a%d" % i, [128, 512], BF16) for i in range(2)]
                    wsb_ = [sbt(st, "p9w%d" % i, [128, 2, 512], BF16) for i in range(2)]
                    psa = [pst(st, "p9pa%d" % i, [128, 512]) for i in range(2)]
                    psg = [pst(st, "p9pg%d" % i, [128, 1024], BF16) for i in range(2)]
                    pso = [pst(st, "p9po%d" % i, [128, 512]) for i in range(2)]
                    puTv = puT[l].rearrange("(kc p) e -> p kc e", p=128)
                    ca = 0
                    co = 0
                    cg = 0
                    for eb in range(NEXP // EB):
                        ub, vb_ = uts[eb % 2], vts[eb % 2]
                        ku, kv_ = 'p9u%d' % (eb % 2), 'p9v%d' % (eb % 2)
                        for kc in range(8):
                            P.dma(ub[:, kc, :], puTv[:, kc, eb * EB:(eb + 1) * EB], writes=[ku], queue='gpsimd')
                        P.dma(vb_[:], pv[l][eb * EB:(eb + 1) * EB, :].rearrange("(c p) d -> p c d", p=128), writes=[kv_], queue='gpsimd')
                        for (g0, gn) in groups:
                            gb = cg % 2
                            cg += 1
                            kg, kw = 'p9g%d' % gb, 'p9w%d' % gb
                            nt_g = gn // 128
                            P.dma(gts[gb][:, 0:nt_g, :],
                                  Gd[tok0 + g0:tok0 + g0 + gn, eb * EB:(eb + 1) * EB].rearrange("(t p) e -> p t e", p=128), writes=[kg])
                            for ec in range(2):
                                ab = ca % 2
                                ca += 1
                                kpa, kpg, ka = 'p9pa%d' % ab, 'p9pg%d' % ab, 'p9a%d' % ab
                                for kc in range(8):
                                    P.mm(psa[ab][:, 0:gn], ub[:, kc, ec * 128:(ec + 1) * 128], hT[:, kc, g0:g0 + gn],
                                         kc == 0, kc == 7, [ku, 'p9h'], [kpa])
                                P.act(asb[ab][:, 0:gn], psa[ab][:, 0:gn], AF.Gelu_apprx_tanh, [kpa], [ka])
                                for t_ in range(nt_g):
                                    P.tr(psg[ab][:, t_ * 128:(t_ + 1) * 128], gts[gb][:, t_, ec * 128:(ec + 1) * 128], identb[:],
                                         [kg, 'identb'], [kpg])
                                P.tt(wsb_[gb][:, ec, 0:gn], asb[ab][:, 0:gn], psg[ab][:, 0:gn], ALU.mult, [ka, kpg], [kw])
                            for t_ in range(nt_g):
                                tl = (g0 // 128) + t_
                                for half in range(2):
                                    ob = co % 2
                                    co += 1
                                    kpo = 'p9po%d' % ob
                                    for ec in range(2):
                                        P.mm(pso[ob][:], wsb_[gb][:, ec, t_ * 128:(t_ + 1) * 128], vb_[:, ec, half * 512:(half + 1) * 512],
                                             ec == 0, ec == 1, [kw, kv_], [kpo])
                                    asl = acc[:, tl, half * 512:(half + 1) * 512]
                                    P.tt(asl, pso[ob][:], asl, ALU.add, [kpo, ('p9acc', tl)], [('p9acc', tl)])
                    xts = [sbt(st, "p9x%d" % i, [128, D]) for i in range(2)]
                    if last:
                        fg = sbt(st, "p9fg", [128, D])
                        P.dma(fg[:], fin_g.partition_broadcast(128)[:, 0, :], writes=['p9fg'])
                        xns = [sbt(st, "p9xn%d" % i, [128, D]) for i in range(2)]
                        sms = [sbt(st, "p9s%d" % i, [128, 4]) for i in range(2)]
                    for tl in range(ntile):
                        b = tl % 2
                        ti = (tok0 // 128) + tl
                        _, is_ctx, who = tile_info(ti)
                        kx = 'p9x%d' % b
                        P.dma(xts[b][:], X[ti * 128:(ti + 1) * 128, :], writes=[kx])
                        P.tt(acc[:, tl, :], acc[:, tl, :], g2b[:, who, :], ALU.mult, [('p9acc', tl), 'g2b'], [('p9acc', tl)])
                        P.tt(xts[b][:], xts[b][:], acc[:, tl, :], ALU.add, [kx, ('p9acc', tl)], [kx])
                        if not last:
                            P.dma(X[ti * 128:(ti + 1) * 128, :], xts[b][:], reads=[kx])
                        else:
                            rmsnorm_tile(None, xts[b], kx, xns[b], 'p9xn%d' % b, sms[b], 'p9s%d' % b)
                            P.tt(xns[b][:], xns[b][:], fg[:], ALU.mult, ['p9xn%d' % b, 'p9fg'], ['p9xn%d' % b])
                            orow = lb * SEQ + tl * 128
                            P.dma(out_d[orow:orow + 128, :], xns[b][:], reads=['p9xn%d' % b])
                    P.end_phase()
        P.end_phase()
    print("instructions recorded:", P.nins, {e: P.cnt[e] for e in ENGS})
    return nc


def _na_bias_tables(rpb):
    out = np.full((8, 128, 5, 640), NEG, np.float32)
    r0s = [0, 2, 4, 28, 30]
    qc = np.arange(64)
    c0 = np.clip(qc - 8, 0, 48)
    kc = np.arange(64)
    col_ok = (kc[None, :] >= c0[:, None]) & (kc[None, :] < c0[:, None] + 16)
    dc = np.clip(kc[None, :] - qc[:, None] + 15, 0, 30)
    for ci, r0 in enumerate(r0s):
        s0 = min(max(r0 - 4, 0), 24)
        u0 = min(s0, 22)
        for dq in range(2):
            r = r0 + dq
            st = min(max(r - 4, 0), 24)
            for kr in range(10):
                krow = u0 + kr
                if not (st <= krow < st + 8):
                    continue
                dr = krow - r + 7
                vals = rpb[:, dr, :][:, dc]
                blk = np.where(col_ok[None], vals, np.float32(NEG))
                out[:, dq * 64:(dq + 1) * 64, ci, kr * 64:(kr + 1) * 64] = blk
    return out


def prep_shared(inp):
    f = np.float32
    sh = {}
    sh["ada_w"] = np.ascontiguousarray(inp["ada_w"], f)
    sh["ada_b"] = np.ascontiguousarray(inp["ada_b"].reshape(DEPTH, 1, 6 * D), f)
    sh["ada_bT"] = np.ascontiguousarray(inp["ada_b"].reshape(DEPTH, 48, 128).transpose(0, 2, 1), f)
    sh["n1g"] = np.ascontiguousarray(inp["norm1_g"].reshape(DEPTH, 8, 128).transpose(0, 2, 1), f)
    sh["n2g"] = np.ascontiguousarray(inp["norm2_g"].reshape(DEPTH, 8, 128).transpose(0, 2, 1), f)
    sh["fin_g"] = np.ascontiguousarray(inp["final_g"].reshape(1, D), f)
    sh["w_in"] = np.ascontiguousarray(inp["w_in"], f)
    sh["w_out"] = np.ascontiguousarray(inp["w_out"], f)
    sh["rw_w_up"] = np.ascontiguousarray(inp["rw_w_up"].reshape(DEPTH, 128, 256), f)
    sh["rw_a_up"] = np.ascontiguousarray(inp["rw_a_up"].reshape(DEPTH, 128, 256), f)
    sh["rw_w0"] = np.ascontiguousarray(inp["rw_w0"].reshape(DEPTH, 1, 512), f)
    sh["rw_a0"] = np.ascontiguousarray(inp["rw_a0"].reshape(DEPTH, 1, 512), f)
    sh["rw_g_up"] = np.ascontiguousarray(inp["rw_g_up"], f)
    sh["rw_k_k"] = np.ascontiguousarray(inp["rw_k_k"].reshape(DEPTH, 1, 256), f)
    sh["rw_k_a"] = np.ascontiguousarray(inp["rw_k_a"].reshape(DEPTH, 1, 256), f)
    rk = inp["rw_r_k"].reshape(DEPTH, 4, 64).transpose(0, 2, 1)
    sh["rw_r_k"] = np.ascontiguousarray(np.concatenate([rk, rk], axis=1), f)
    lx = inp["rw_lnx_g"].reshape(DEPTH, 4, 64).transpose(0, 2, 1)
    sh["rw_lnx"] = np.ascontiguousarray(np.concatenate([lx, lx], axis=1), f)
    sh["na_bias"] = np.stack([_na_bias_tables(np.asarray(inp["na_rpb"][l], f)) for l in range(DEPTH)])
    sh["sc_w"] = np.ascontiguousarray(inp["sc_conv_w"].reshape(DEPTH, 3, 2, 128).transpose(0, 3, 2, 1), f)
    sh["pq_w"] = np.ascontiguousarray(inp["peer_q_w"], f)
    sh["pkT"] = np.ascontiguousarray(inp["peer_sub_keys"].reshape(DEPTH, 16, 128, 128).transpose(0, 3, 1, 2), f)
    sh["puT"] = np.ascontiguousarray(inp["peer_u"].transpose(0, 2, 1), f)
    sh["pv"] = np.ascontiguousarray(inp["peer_v"], f)
    sh["ident"] = np.eye(128, dtype=f)
    jb = np.zeros((128, 128), f)
    jb[:64, :64] = 1.0
    jb[64:, 64:] = 1.0
    sh["jblk"] = jb
    return sh


def prep_core(inp, core):
    f = np.float32
    b0 = core * LB
    xs = []
    for lb in range(LB):
        xs.append(inp["ctx"][b0 + lb])
        xs.append(inp["x"][b0 + lb])
    m = {"xin": np.ascontiguousarray(np.concatenate(xs, axis=0), f)}
    cv = np.stack([inp["c"][b0], inp["c"][b0 + 1], inp["c_ctx"]], axis=0)
    m["cT"] = np.ascontiguousarray(cv.reshape(3, 8, 128).transpose(2, 1, 0), f)
    return m


_NC_CACHE = {}


def kernel(**inputs):
    inp = {k: np.asarray(v) for k, v in inputs.items()}
    if 'nc' not in _NC_CACHE:
        _NC_CACHE['nc'] = build()
    nc = _NC_CACHE['nc']
    sh = prep_shared(inp)
    in_maps = []
    for core in range(8):
        m = dict(sh)
        m.update(prep_core(inp, core))
        in_maps.append(m)
    res = run_bass_kernel_spmd(nc, in_maps, core_ids=list(range(8)))
    outs = [np.asarray(r["out"]).reshape(LB, SEQ, D) for r in res.results]
    return np.concatenate(outs, axis=0).astype(np.float32)
```
